# Optimizing a Trainium2 kernel written in Bass

```python
import math
import jax, jax.numpy as jnp
from jax import lax
import numpy as np

D_MODEL = 1024
BATCH = 8
SEQ = 4096
DEPTH = 4

CONV_A_WIDTH = 512
CONV_A_KERNEL = 31
CONV_B_WIDTH = 512
CONV_B_KERNEL = 3
N_HEADS = 8
HEAD_DIM = 64
V_HEAD_DIM = 2 * HEAD_DIM
ATTN_QK_WIDTH = N_HEADS * 2 * HEAD_DIM
ATTN_V_WIDTH = N_HEADS * V_HEAD_DIM
Q_BLOCK = 128
ROPE_THETA = 10000.0
N_BRANCHES = 3
D_FF = 4 * D_MODEL
N_ADA = 6
NORM_EPS = 1e-6

IN_SIZES = (2 * CONV_A_WIDTH,
            3 * CONV_B_WIDTH,
            ATTN_QK_WIDTH,
            ATTN_QK_WIDTH,
            ATTN_V_WIDTH,
            N_BRANCHES * D_MODEL)
D_IN_PROJ = sum(IN_SIZES)
IN_SPLITS = tuple(int(s) for s in np.cumsum(IN_SIZES)[:-1])

kernel_name = "hybrid_conv_shortconv_diffattn_block"


def rmsnorm(x, g):
    xf = x.astype(jnp.float32)
    y = xf * lax.rsqrt(jnp.mean(xf * xf, axis=-1, keepdims=True) + NORM_EPS)
    return (y * g.astype(jnp.float32)).astype(x.dtype)


def layernorm(x, g, b):
    xf = x.astype(jnp.float32)
    mu = jnp.mean(xf, axis=-1, keepdims=True)
    var = jnp.mean(jnp.square(xf - mu), axis=-1, keepdims=True)
    y = (xf - mu) * lax.rsqrt(var + NORM_EPS)
    return (y * g.astype(jnp.float32) + b.astype(jnp.float32)).astype(x.dtype)


def modulate(h, shift, scale):
    return h * (1.0 + scale[:, None, :]) + shift[:, None, :]


def causal_depthwise_conv(x, w):
    k = w.shape[0]
    return lax.conv_general_dilated(
        x, w[:, None, :].astype(x.dtype), window_strides=(1,), padding=[(k - 1, 0)],
        dimension_numbers=('NWC', 'WIO', 'NWC'), feature_group_count=x.shape[-1])


def rope_tables(positions, dtype):
    inv_freq = ROPE_THETA ** (-jnp.arange(0, HEAD_DIM, 2, dtype=jnp.float32) / HEAD_DIM)
    ang = positions.astype(jnp.float32)[..., None] * inv_freq
    cos = jnp.cos(ang)[:, :, None, None, :].astype(dtype)
    sin = jnp.sin(ang)[:, :, None, None, :].astype(dtype)
    return cos, sin


def apply_rope(t, cos, sin):
    t1, t2 = jnp.split(t, 2, axis=-1)
    return jnp.concatenate([t1 * cos - t2 * sin, t2 * cos + t1 * sin], axis=-1)


def diff_attention(q, k, v, lam, subln_g, lambda_init):
    bsz, seq = q.shape[0], q.shape[1]
    scale = HEAD_DIM ** -0.5
    outs = []
    for i in range(seq // Q_BLOCK):
        lo, hi = i * Q_BLOCK, (i + 1) * Q_BLOCK
        qb, kb, vb = q[:, lo:hi], k[:, :hi], v[:, :hi]
        s = jnp.einsum('bqhcd,bkhcd->bhcqk', qb, kb).astype(jnp.float32) * scale
        mask = (lo + jnp.arange(Q_BLOCK))[:, None] >= jnp.arange(hi)[None, :]
        p = jax.nn.softmax(jnp.where(mask, s, -jnp.inf), axis=-1)
        a = p[:, :, 0] - lam * p[:, :, 1]
        outs.append(jnp.einsum('bhqk,bkhd->bqhd', a.astype(vb.dtype), vb))
    o = jnp.concatenate(outs, axis=1)
    o = rmsnorm(o, subln_g) * (1.0 - lambda_init)
    return o.reshape(bsz, seq, N_HEADS * V_HEAD_DIM)


def setup_inputs(seed: int = 0) -> dict:
    key = jax.random.key(seed)
    ks = iter(jax.random.split(key, 32))
    f32 = jnp.float32

    def nrm(shape, scale):
        return jax.random.normal(next(ks), shape, f32) * scale

    def gain(shape):
        return 1.0 + nrm(shape, 0.02)

    L, D = DEPTH, D_MODEL
    x = jax.random.normal(next(ks), (BATCH, SEQ, D), f32)
    c = jax.random.normal(next(ks), (BATCH, D), f32)
    offset = jax.random.randint(next(ks), (BATCH, 1), 0, SEQ, dtype=jnp.int32)
    positions = (offset + jnp.arange(SEQ, dtype=jnp.int32)[None, :]).astype(jnp.int32)
    return {
        "x": x,
        "c": c,
        "positions": positions,
        "w_ada": nrm((L, D, N_ADA * D), 0.5 * D ** -0.5),
        "b_ada": nrm((L, N_ADA * D), 0.02),
        "norm_mix_g": gain((L, D)),
        "w_in": nrm((L, D, D_IN_PROJ), D ** -0.5),
        "conv_a_w": nrm((L, CONV_A_KERNEL, CONV_A_WIDTH), CONV_A_KERNEL ** -0.5),
        "conv_a_b": nrm((L, CONV_A_WIDTH), 0.02),
        "ln_a_g": gain((L, CONV_A_WIDTH)),
        "ln_a_b": nrm((L, CONV_A_WIDTH), 0.02),
        "w_a_out": nrm((L, CONV_A_WIDTH, D), CONV_A_WIDTH ** -0.5),
        "conv_b_w": nrm((L, CONV_B_KERNEL, CONV_B_WIDTH), CONV_B_KERNEL ** -0.5),
        "w_b_out": nrm((L, CONV_B_WIDTH, D), CONV_B_WIDTH ** -0.5),
        "lam_q1": nrm((L, HEAD_DIM), 0.1),
        "lam_k1": nrm((L, HEAD_DIM), 0.1),
        "lam_q2": nrm((L, HEAD_DIM), 0.1),
        "lam_k2": nrm((L, HEAD_DIM), 0.1),
        "subln_g": gain((L, V_HEAD_DIM)),
        "w_c_out": nrm((L, ATTN_V_WIDTH, D), ATTN_V_WIDTH ** -0.5),
        "w_out": nrm((L, D, D), D ** -0.5),
        "norm_mlp_g": gain((L, D)),
        "w_ff1": nrm((L, D, D_FF), D ** -0.5),
        "w_ff2": nrm((L, D_FF, D), D_FF ** -0.5),
        "final_g": gain((D,)),
    }


def reference(x, c, positions, w_ada, b_ada, norm_mix_g, w_in, conv_a_w, conv_a_b,
              ln_a_g, ln_a_b, w_a_out, conv_b_w, w_b_out, lam_q1, lam_k1, lam_q2,
              lam_k2, subln_g, w_c_out, w_out, norm_mlp_g, w_ff1, w_ff2, final_g):
    bsz, seq, _ = x.shape
    cos, sin = rope_tables(positions, x.dtype)
    c_act = jax.nn.silu(c)
    for l in range(DEPTH):
        lambda_init = 0.8 - 0.6 * math.exp(-0.3 * l)
        ada = jnp.einsum('bd,de->be', c_act, w_ada[l]) + b_ada[l]
        sh_m, sc_m, g_m, sh_f, sc_f, g_f = jnp.split(ada, N_ADA, axis=-1)

        h = modulate(rmsnorm(x, norm_mix_g[l]), sh_m, sc_m)
        proj = jnp.einsum('bsd,de->bse', h, w_in[l])
        a_in, b_in, q, k, v, gate_pre = jnp.split(proj, IN_SPLITS, axis=-1)

        a = jax.nn.glu(a_in, axis=-1)
        a = causal_depthwise_conv(a, conv_a_w[l]) + conv_a_b[l]
        a = jax.nn.silu(layernorm(a, ln_a_g[l], ln_a_b[l]))
        y_a = jnp.einsum('bsc,cd->bsd', a, w_a_out[l])

        bg, cg, xb = jnp.split(b_in, 3, axis=-1)
        y_b = jnp.einsum('bsc,cd->bsd', bg * causal_depthwise_conv(cg * xb, conv_b_w[l]), w_b_out[l])

        q = apply_rope(q.reshape(bsz, seq, N_HEADS, 2, HEAD_DIM), cos, sin)
        k = apply_rope(k.reshape(bsz, seq, N_HEADS, 2, HEAD_DIM), cos, sin)
        v = v.reshape(bsz, seq, N_HEADS, V_HEAD_DIM)
        lam = (jnp.exp(jnp.sum(lam_q1[l].astype(jnp.float32) * lam_k1[l].astype(jnp.float32)))
               - jnp.exp(jnp.sum(lam_q2[l].astype(jnp.float32) * lam_k2[l].astype(jnp.float32)))
               + lambda_init)
        o = diff_attention(q, k, v, lam, subln_g[l], lambda_init)
        y_c = jnp.einsum('bsc,cd->bsd', o, w_c_out[l])

        g_a, g_b, g_c = jnp.split(jax.nn.sigmoid(gate_pre), N_BRANCHES, axis=-1)
        merged = g_a * y_a + g_b * y_b + g_c * y_c
        x = x + g_m[:, None, :] * jnp.einsum('bsd,de->bse', merged, w_out[l])

        h = modulate(rmsnorm(x, norm_mlp_g[l]), sh_f, sc_f)
        f = jnp.square(jax.nn.relu(jnp.einsum('bsd,df->bsf', h, w_ff1[l])))
        x = x + g_f[:, None, :] * jnp.einsum('bsf,fd->bsd', f, w_ff2[l])
    return rmsnorm(x, final_g)
```

```python
import math
import numpy as np
import concourse.bass as bass
import concourse.mybir as mybir
from concourse.bass_utils import run_bass_kernel_spmd

F32, BF16, I32 = mybir.dt.float32, mybir.dt.bfloat16, mybir.dt.int32
AF = mybir.ActivationFunctionType
ALU = mybir.AluOpType
AX = mybir.AxisListType

D = 1024
NCH = 8
T = 512
DFF = 4096
DIN = 8704
NG = 39
NPV = 8 + 8 + 124 + 4 + 4 + 4 + 12 + 1 + 48
PV_GMIX, PV_GMLP, PV_CAW, PV_CAB, PV_LNG, PV_LNB, PV_CBW, PV_SUBG, PV_BADA = 0, 8, 16, 140, 144, 148, 152, 164, 165


class Buf:
    __slots__ = ("name", "w", "rd", "rd_dma")

    def __init__(self, name):
        self.name = name
        self.w = None
        self.rd = {}
        self.rd_dma = []


class Ins:
    __slots__ = ("eng", "fn", "deps", "sig", "idx", "dma", "ev", "seq")


class Prog:
    ENGS = ("pe", "act", "dve", "pool", "sp")
    RING = 8

    def __init__(self):
        self.lists = {e: [] for e in self.ENGS}
        self.dma_n = {e: 0 for e in self.ENGS}
        self.dma_hist = {e: [] for e in self.ENGS}
        self.seq = 0

    def _add(self, eng, fn, reads, writes, dma):
        ins = Ins()
        ins.eng, ins.fn, ins.sig, ins.idx, ins.dma, ins.ev = eng, fn, False, 0, dma, None
        ins.seq = self.seq
        self.seq += 1
        deps = {}
        for b in reads:
            if b.w is not None:
                deps[id(b.w)] = b.w
        for b in writes:
            if b.w is not None:
                deps[id(b.w)] = b.w
            for r in b.rd.values():
                deps[id(r)] = r
            for r in b.rd_dma:
                deps[id(r)] = r
        if dma:
            k = self.dma_n[eng]
            self.dma_n[eng] = k + 1
            ins.ev = (eng, k % self.RING, 16 * (k // self.RING + 1))
            hist = self.dma_hist[eng]
            if k >= self.RING:
                p = hist[k - self.RING]
                deps[id(p)] = p
            hist.append(ins)
        for b in reads:
            if dma:
                b.rd_dma.append(ins)
            else:
                b.rd[eng] = ins
        for b in writes:
            b.w = ins
            b.rd = {}
            b.rd_dma = []
        dl = []
        for d in deps.values():
            if d is ins:
                continue
            if (not d.dma) and (not dma) and d.eng == "pe" and eng == "pe":
                continue
            if not d.dma:
                d.sig = True
            dl.append(d)
        ins.deps = dl
        self.lists[eng].append(ins)
        return ins

    def op(self, eng, fn, reads=(), writes=()):
        return self._add(eng, fn, reads, writes, False)

    def dma(self, q, out, in_, reads=(), writes=()):
        return self._add(q, lambda E: E.dma_start(out=out, in_=in_), reads, writes, True)

    @staticmethod
    def barrier(new_bufs, old_bufs):
        rd = {}
        rd_dma = []
        for b in old_bufs:
            cands = list(b.rd.values())
            if b.w is not None:
                if b.w.dma:
                    rd_dma.append(b.w)
                else:
                    cands.append(b.w)
            for c in cands:
                if c.eng not in rd or rd[c.eng].seq < c.seq:
                    rd[c.eng] = c
            rd_dma.extend(b.rd_dma)
        for nb in new_bufs:
            nb.w = None
            nb.rd = dict(rd)
            nb.rd_dma = list(rd_dma)

    def emit(self, nc, block, sems, dsems):
        for e in self.ENGS:
            n = 0
            for ins in self.lists[e]:
                if ins.sig and not ins.dma:
                    n += 1
                    ins.idx = n

        def event(d):
            if d.dma:
                q, slot, val = d.ev
                return dsems[q][slot], val
            return sems[d.eng], d.idx

        def body_for(ename):
            def body(E):
                waited = {}
                for ins in self.lists[ename]:
                    for d in ins.deps:
                        sem, val = event(d)
                        if waited.get(id(sem), 0) < val:
                            E.wait_ge(sem, val)
                            waited[id(sem)] = val
                    r = ins.fn(E)
                    if ins.dma:
                        r.then_inc(dsems[ename][ins.ev[1]], 16)
                    elif ins.sig:
                        r.then_inc(sems[ename], 1)
                hist = self.dma_hist[ename]
                for ins in hist[-self.RING:]:
                    sem, val = event(ins)
                    if waited.get(id(sem), 0) < val:
                        E.wait_ge(sem, val)
                        waited[id(sem)] = val
            return body

        block.tensor(body_for("pe"))
        block.scalar(body_for("act"))
        block.vector(body_for("dve"))
        block.gpsimd(body_for("pool"))
        block.sync(body_for("sp"))


def group_table():
    g = {}
    for i in range(17):
        g[i] = ("w_in", 0, 8, i * 512, 512)
    g[17] = ("w_a_out", 0, 4, 0, 1024)
    g[18] = ("w_b_out", 0, 4, 0, 1024)
    g[19] = ("w_c_out", 0, 8, 0, 512)
    g[20] = ("w_c_out", 0, 8, 512, 512)
    g[21] = ("w_out", 0, 8, 0, 512)
    g[22] = ("w_out", 0, 8, 512, 512)
    for i in range(8):
        g[23 + i] = ("w_ff1", 0, 8, i * 512, 512)
    for hf in range(2):
        for q in range(4):
            g[31 + hf * 4 + q] = ("w_ff2", q * 1024, 8, hf * 512, 512)
    return g


TILE_SEQ = ([1, 0, 3, 4, 2, 5, 6, 7, 8, 9, 10] +
            [11, 17, 12, 17, 13, 18, 14, 18, 15, 19, 16, 20, 21, 22] +
            list(range(23, 31)) + list(range(31, 39)))


class _Stop(Exception):
    pass


USE_WCACHE = False


def build_program(S, L, final_norm=True, lambda_layer0=0, debug_stop=None, debug_tile=0):
    NT = S // T
    NTB = S // 128
    nc = bass.Bass("TRN2", target_bir_lowering=False)
    P = Prog()
    GT = group_table()

    def din(name, shape, dt=F32):
        return nc.dram_tensor(name, shape, dt, kind="ExternalInput").ap()

    xT = din("xT", [D, S])
    cT = din("cT", [128, NCH])
    posr = din("posr", [128, S], I32)
    consts = din("consts", [128, 3 * 128 + 1])
    pvec = din("pvec", [L, 128, NPV])
    lamv = din("lamv", [L, 128, 256])
    finalg = din("finalg", [128, NCH])
    wd = {
        "w_ada": din("w_ada", [L, D, 6 * D]),
        "w_in": din("w_in", [L, D, DIN]),
        "w_a_out": din("w_a_out", [L, 512, D]),
        "w_b_out": din("w_b_out", [L, 512, D]),
        "w_c_out": din("w_c_out", [L, D, D]),
        "w_out": din("w_out", [L, D, D]),
        "w_ff1": din("w_ff1", [L, D, DFF]),
        "w_ff2": din("w_ff2", [L, DFF, D]),
    }
    outT = nc.dram_tensor("outT", [D, S], F32, kind="ExternalOutput").ap()

    def dscr(name, shape, dt):
        return nc.dram_tensor(name, shape, dt, kind="Internal").ap()

    xres = dscr("xres", [D, S], F32)
    wsc = dscr("wsc", [L * NG, 128, 4096], BF16)
    kscr = dscr("kscr", [NCH, 128, S], BF16)
    vscr = dscr("vscr", [8, 128, NTB * 129], BF16)
    csscr = dscr("csscr", [2, 128, S], F32)

    off = [20 * 1024]

    def sb(name, shape, dt, at=None):
        esz = 2 if dt == BF16 else 4
        nbytes = esz * int(np.prod(shape[1:]))
        nbytes = (nbytes + 63) // 64 * 64
        if at is None:
            at = off[0]
            off[0] += nbytes
        t = nc.alloc_sbuf_tensor_at(name, list(shape), dt, offset=at)
        return t, at, nbytes

    wr = [sb(f"wr{i}", [128, 4096], BF16)[0] for i in range(4)]
    wrB = [Buf(f"wr{i}") for i in range(4)]
    ws = [sb(f"ws{i}", [128, 2048], F32)[0] for i in range(2)]
    wsB = [Buf(f"ws{i}") for i in range(2)]
    xt = sb("xt", [128, NCH, T], F32)[0]
    xtB = [Buf(f"xt{c}") for c in range(NCH)]
    ht = sb("ht", [128, NCH, T], BF16)[0]
    htB = [Buf(f"ht{c}") for c in range(NCH)]
    sq = sb("sq", [128, NCH, T], BF16)[0]
    sqB = [Buf(f"sq{c}") for c in range(NCH)]
    rstd = sb("rstd", [128, T], F32)[0]
    rstdB = Buf("rstd")
    NT32 = 4
    t32 = [sb(f"t32_{i}", [128, T], F32)[0] for i in range(NT32)]
    t32B = [Buf(f"t32_{i}") for i in range(NT32)]
    cs = sb("cs", [128, T], F32)[0]
    sn = sb("sn", [128, T], F32)[0]
    csB, snB = Buf("cs"), Buf("sn")
    pv = sb("pv", [128, NPV], F32)[0]
    pvB = Buf("pv")
    lamt = sb("lamt", [128, 256], F32)[0]
    lamp = sb("lamp", [128, 2, 64], F32)[0]
    lams = sb("lams", [128, 8], F32)[0]
    lamB = Buf("lam")
    ada = sb("ada", [128, 48], F32)[0]
    gm = sb("gm", [128, 16], F32)[0]
    adaB = Buf("ada")
    cact = sb("cact", [128, NCH], F32)[0]
    cactB = Buf("cact")
    cst = sb("cst", [128, 3 * 128 + 1], F32)[0]
    cstB = Buf("cst")
    onesD = sb("onesD", [128, 128], BF16)[0]
    ones512 = sb("ones512", [128, 128], F32)[0]
    ident = sb("ident", [128, 128], BF16)[0]
    perm = sb("perm", [128, 128], BF16)[0]
    tri = sb("tri", [128, 128], BF16)[0]
    epsb = sb("epsb", [128, 1], F32)[0]
    negpi = sb("negpi", [128, 1], F32)[0]
    fg = sb("fg", [128, NCH], F32)[0]
    constB = Buf("const")
    ahalo = sb("ahalo", [128, 4, 30], F32)[0]
    uhalo = sb("uhalo", [128, 4, 2], F32)[0]
    ahB = [Buf(f"ah{c}") for c in range(4)]
    uhB = [Buf(f"uh{c}") for c in range(4)]
    pint = sb("pint", [128, T], I32)[0]
    pintB = Buf("pint")
    kr = [sb(f"kr{i}", [128, T], BF16)[0] for i in range(4)]
    vr = [sb(f"vr{i}", [128, 4 * 129], BF16)[0] for i in range(4)]
    krB = [Buf(f"kr{i}") for i in range(4)]
    vrB = [Buf(f"vr{i}") for i in range(4)]
    pT = [sb(f"pT{i}", [128, T], BF16)[0] for i in range(3)]
    pTB = [Buf(f"pT{i}") for i in range(3)]
    ofin = sb("ofin", [128, 2, 128], F32)[0]
    ofinB = [Buf("ofin0"), Buf("ofin1")]
    osq = sb("osq", [128, 128], F32)[0]
    osqB = Buf("osq")
    osm = sb("osm", [128, 8], F32)[0]
    osmB = Buf("osm")
    onb = [sb(f"onb{i}", [128, 128], BF16)[0] for i in range(2)]
    onbB = [Buf("onb0"), Buf("onb1")]
    carry0 = off[0]
    Qm = [sb(f"Qm{i}", [128, NCH, T], BF16)[0] for i in range(2)]
    QmB = [[Buf(f"Qm{i}_{c}") for c in range(NCH)] for i in range(2)]
    oT = sb("oT", [128, 8, T], BF16)[0]
    oTB = [Buf(f"oT{h}") for h in range(8)]
    aact = sb("aact", [128, 4, T], BF16)[0]
    aactB = [Buf(f"aact{c}") for c in range(4)]
    bmix = sb("bmix", [128, 4, T], BF16)[0]
    bmixB = [Buf(f"bmix{c}") for c in range(4)]
    carry_end = off[0]
    ft = sb("ft", [128, 32, T], BF16, at=carry0)[0]
    ftB = [Buf(f"ft{c}") for c in range(32)]
    assert carry_end - carry0 == 32 * T * 2, (carry_end - carry0)
    carryB = [b for q in QmB for b in q] + oTB + aactB + bmixB
    arena0 = off[0]
    abuf = sb("abuf", [128, 4, 30 + T], F32)[0]
    abufB = [Buf(f"abuf{c}") for c in range(4)]
    acc = sb("acc", [128, 4, T], F32)[0]
    accB = [Buf(f"acc{c}") for c in range(4)]
    cgs = sb("cgs", [128, 4, T], F32)[0]
    cgsB = [Buf(f"cgs{c}") for c in range(4)]
    ubuf = sb("ubuf", [128, 4, 2 + T], F32)[0]
    ubufB = [Buf(f"ubuf{c}") for c in range(4)]
    qtmp = [sb(f"qtmp{i}", [128, T], BF16)[0] for i in range(2)]
    qtmpB = [Buf("qtmp0"), Buf("qtmp1")]
    kt = sb("kt", [128, NCH, T], BF16)[0]
    ktB = [Buf(f"kt{c}") for c in range(NCH)]
    vt = sb("vt", [128, 8, 4, 129], BF16)[0]
    vtB = [Buf(f"vt{i}") for i in range(8)]
    arena_end = off[0]
    mg = sb("mg", [128, NCH, T], F32, at=arena0)[0]
    mgb = sb("mgb", [128, NCH, T], BF16, at=arena0 + NCH * T * 4)[0]
    mgB = [Buf(f"mg{c}") for c in range(NCH)]
    mgbB = [Buf(f"mgb{c}") for c in range(NCH)]
    assert arena_end - arena0 >= NCH * T * 6
    arenaB = abufB + accB + cgsB + ubufB + qtmpB + ktB + vtB
    assert off[0] <= 224 * 1024, off[0]

    ps = [nc.alloc_psum_tensor(f"ps{i}", [128, 512], F32) for i in range(7)]
    psB = [Buf(f"ps{i}") for i in range(7)]
    ptr = nc.alloc_psum_tensor("ptr", [128, 1024], BF16)
    ptrB = Buf("ptr")
    ps_rr = [0]

    def next_ps():
        i = ps_rr[0] % 7
        ps_rr[0] += 1
        return ps[i], psB[i]

    t32_rr = [0]

    def next_t32():
        i = t32_rr[0] % NT32
        t32_rr[0] += 1
        return t32[i], t32B[i]

    xresB = [Buf(f"xres{j}") for j in range(NT)]
    kregB = [Buf(f"kreg{j}") for j in range(NT)]
    vregB = [Buf(f"vreg{j}") for j in range(NT)]
    csregB = [Buf(f"csreg{j}") for j in range(NT)]
    wscB = {}

    def MM(out, lhsT, rhs, start, stop, rd, wr_):
        P.op("pe", lambda E: E.matmul(out, lhsT, rhs, start=start, stop=stop), rd, wr_)

    def ACT(out, in_, func, rd, wr_, **kw):
        P.op("act", lambda E: E.activation(out=out, in_=in_, func=func, **kw), rd, wr_)

    def TT(eng, out, in0, in1, op, rd, wr_):
        P.op(eng, lambda E: E.tensor_tensor(out=out, in0=in0, in1=in1, op=op), rd, wr_)

    def TS(eng, out, in0, s1, s2, op0, op1, rd, wr_):
        P.op(eng, lambda E: E.tensor_scalar(out=out, in0=in0, scalar1=s1, scalar2=s2, op0=op0, op1=op1), rd, wr_)

    def TSS(eng, out, in_, s, op, rd, wr_):
        P.op(eng, lambda E: E.tensor_single_scalar(out=out, in_=in_, scalar=s, op=op), rd, wr_)

    def STT(eng, out, in0, scalar, in1, op0, op1, rd, wr_):
        P.op(eng, lambda E: E.scalar_tensor_tensor(out=out, in0=in0, scalar=scalar, in1=in1, op0=op0, op1=op1), rd, wr_)

    def CP(eng, out, in_, rd, wr_):
        P.op(eng, lambda E: E.tensor_copy(out=out, in_=in_), rd, wr_)

    def RECIP(out, in_, rd, wr_):
        P.op("dve", lambda E: E.reciprocal(out=out, in_=in_), rd, wr_)

    def RSUM(out, in_, rd, wr_):
        P.op("dve", lambda E: E.reduce_sum(out=out, in_=in_, axis=AX.X), rd, wr_)

    def MEMSET(eng, ap, val, wr_):
        P.op(eng, lambda E: E.memset(ap, val), (), wr_)

    use_seq = []
    for l in range(L):
        for J in range(NT):
            for gid in TILE_SEQ:
                use_seq.append((l, J, gid))
    wstate = {"loaded": 0, "used": 0, "stage": 0}
    LOOKAHEAD = 2

    def src_piece(l, gid, hh):
        name, row0, nkc, col0, width = GT[gid]
        w = wd[name][l]
        hk = nkc // 2
        r0 = row0 + hh * hk * 128
        return w[r0:r0 + hk * 128, col0:col0 + width].rearrange("(k p) n -> p k n", p=128), hk, width

    def record_load(i):
        l, J, gid = use_seq[i]
        slot = i % 4
        key = (l, gid)
        if (key not in wscB) or not USE_WCACHE:
            wscB[key] = Buf(f"wsc{l}_{gid}")
            for hh in range(2):
                src, hk, width = src_piece(l, gid, hh)
                si = wstate["stage"] % 2
                wstate["stage"] += 1
                P.dma("sp", ws[si][:, :].rearrange("p (k n) -> p k n", k=hk), src, (), (wsB[si],))
                CP("pool", wr[slot][:, hh * 2048:(hh + 1) * 2048], ws[si][:, :], (wsB[si],), (wrB[slot],))
            if USE_WCACHE:
                P.dma("pool", wsc[l * NG + gid], wr[slot][:, :], (wrB[slot],), (wscB[key],))
        else:
            P.dma("sp", wr[slot][:, :], wsc[l * NG + gid], (wscB[key],), (wrB[slot],))

    def get_group(l, J, gid):
        i = wstate["used"]
        assert use_seq[i] == (l, J, gid), (use_seq[i], (l, J, gid))
        lim = min(i + 1 + LOOKAHEAD, len(use_seq))
        while wstate["loaded"] < lim:
            record_load(wstate["loaded"])
            wstate["loaded"] += 1
        wstate["used"] = i + 1
        slot = i % 4
        return wr[slot], wrB[slot]

    P.dma("sp", cst[:, :], consts, (), (cstB,))
    CP("dve", ident[:, :], cst[:, 0:128], (cstB,), (constB,))
    CP("dve", perm[:, :], cst[:, 128:256], (cstB,), (constB,))
    CP("dve", tri[:, :], cst[:, 256:384], (cstB,), (constB,))
    MEMSET("pool", onesD[:, :], 1.0 / D, (constB,))
    MEMSET("pool", ones512[:, :], 1.0 / 512, (constB,))
    MEMSET("pool", epsb[:, :], 1e-6, (constB,))
    MEMSET("pool", negpi[:, :], -math.pi, (constB,))
    P.dma("sp", fg[:, :], finalg, (), (constB,))
    P.dma("sp", cact[:, :], cT, (), (cactB,))
    ACT(cact[:, :], cact[:, :], AF.Silu, (cactB,), (cactB,))
    invf = cst[:, 384:385]
    for j in range(NT):
        P.dma("sp", pint[:, :], posr[:, j * T:(j + 1) * T], (), (pintB,))
        yb, ybB = next_t32()
        CP("dve", yb[:, :], pint[:, :], (pintB,), (ybB,))
        for which, addc in ((0, 0.75), (1, 0.5)):
            y2, y2B = next_t32()
            TS("dve", y2[:, :], yb[:, :], invf, addc, ALU.mult, ALU.add, (ybB, cstB), (y2B,))
            CP("dve", pint[:, :], y2[:, :], (y2B,), (pintB,))
            y3, y3B = next_t32()
            CP("dve", y3[:, :], pint[:, :], (pintB,), (y3B,))
            TT("dve", y2[:, :], y2[:, :], y3[:, :], ALU.subtract, (y2B, y3B), (y2B,))
            TSS("dve", y3[:, :], y2[:, :], 0.0, ALU.is_lt, (y2B,), (y3B,))
            TT("dve", y2[:, :], y2[:, :], y3[:, :], ALU.add, (y2B, y3B), (y2B,))
            ACT(y2[:, :], y2[:, :], AF.Sin, (y2B, constB), (y2B,), scale=2 * math.pi, bias=negpi[:, :])
            P.dma("sp", csscr[which][:, j * T:(j + 1) * T], y2[:, :], (y2B,), (csregB[j],))
    for c in range(4):
        MEMSET("pool", ahalo[:, c, :], 0.0, (ahB[c],))
        MEMSET("pool", uhalo[:, c, :], 0.0, (uhB[c],))

    def rms_stats(J):
        for c in range(NCH):
            ACT(sq[:, c, :], xt[:, c, :], AF.Square, (xtB[c],), (sqB[c],))
        pn, pnB = next_ps()
        for c in range(NCH):
            MM(pn[:, :], onesD[:, :], sq[:, c, :], c == 0, c == NCH - 1, (constB, sqB[c]), (pnB,))
        ACT(rstd[:, :], pn[:, :], AF.Sqrt, (pnB, constB), (rstdB,), bias=epsb[:, :])
        RECIP(rstd[:, :], rstd[:, :], (rstdB,), (rstdB,))

    def norm_mod(J, gcol, shcol):
        rms_stats(J)
        for c in range(NCH):
            tb, tbB = next_t32()
            STT("dve", tb[:, :], xt[:, c, :], gm[:, gcol + c:gcol + c + 1], rstd[:, :], ALU.mult, ALU.mult,
                (xtB[c], adaB, rstdB), (tbB,))
            TSS("pool", ht[:, c, :], tb[:, :], ada[:, shcol + c:shcol + c + 1], ALU.add, (tbB, adaB), (htB[c],))

    def proj(w, wB, cc, width=512, nkc=8, rhs_t=None, rhsB=None, coff=0):
        p_, pB = next_ps()
        rhs_t = ht if rhs_t is None else rhs_t
        rhsB = htB if rhsB is None else rhsB
        for kc in range(nkc):
            MM(p_[:, :], w[:, kc * width + coff + cc * 128: kc * width + coff + cc * 128 + 128], rhs_t[:, kc, :],
               kc == 0, kc == nkc - 1, (wB, rhsB[kc]), (pB,))
        return p_, pB

    try:
      for l in range(L):
          lam_init = 0.8 - 0.6 * math.exp(-0.3 * (l + lambda_layer0))
          P.dma("sp", pv[:, :], pvec[l], (), (pvB,))
          P.dma("sp", lamt[:, :], lamv[l], (), (lamB,))
          TT("dve", lamp[:, :, :], lamt[:, 0:128].rearrange("p (a b) -> p a b", a=2),
             lamt[:, 128:256].rearrange("p (a b) -> p a b", a=2), ALU.mult, (lamB,), (lamB,))
          RSUM(lams[:, 0:2], lamp[:, :, :], (lamB,), (lamB,))
          ACT(lams[:, 0:2], lams[:, 0:2], AF.Exp, (lamB,), (lamB,))
          TT("dve", lams[:, 2:3], lams[:, 0:1], lams[:, 1:2], ALU.subtract, (lamB,), (lamB,))
          TS("dve", lams[:, 3:4], lams[:, 2:3], lam_init, -1.0, ALU.add, ALU.mult, (lamB,), (lamB,))
          TSS("dve", lams[:, 4:5], pv[:, PV_SUBG:PV_SUBG + 1], 1.0 - lam_init, ALU.mult, (pvB, lamB), (lamB,))
          nlam = lams[:, 3:4]
          sgc = lams[:, 4:5]
          pa, paB = next_ps()
          for pc in range(24):
              si = wstate["stage"] % 2
              wstate["stage"] += 1
              src = wd["w_ada"][l][:, pc * 256:(pc + 1) * 256].rearrange("(k p) n -> p k n", p=128)
              P.dma("sp", ws[si][:, :].rearrange("p (k n) -> p k n", k=8), src, (), (wsB[si],))
              for cc in range(2):
                  j = pc * 2 + cc
                  for kc in range(8):
                      MM(pa[:, j:j + 1], ws[si][:, kc * 256 + cc * 128: kc * 256 + cc * 128 + 128], cact[:, kc:kc + 1],
                         kc == 0, kc == 7, (wsB[si], cactB), (paB,))
          TT("dve", ada[:, :], pa[:, 0:48], pv[:, PV_BADA:PV_BADA + 48], ALU.add, (paB, pvB), (adaB,))
          STT("dve", gm[:, 0:8], ada[:, 8:16], 1.0, pv[:, PV_GMIX:PV_GMIX + 8], ALU.add, ALU.mult, (adaB, pvB), (adaB,))
          STT("dve", gm[:, 8:16], ada[:, 32:40], 1.0, pv[:, PV_GMLP:PV_GMLP + 8], ALU.add, ALU.mult, (adaB, pvB), (adaB,))
          for c in range(4):
              MEMSET("pool", ahalo[:, c, :], 0.0, (ahB[c],))
              MEMSET("pool", uhalo[:, c, :], 0.0, (uhB[c],))

          xsrc = xT if l == 0 else xres
          last = (l == L - 1)

          for J in range(NT):
              tsl = slice(J * T, (J + 1) * T)
              Prog.barrier(arenaB, mgB + mgbB)
              Prog.barrier(carryB, ftB)
              P.dma("sp", xt[:, :, :], xsrc.rearrange("(c p) t -> p c t", p=128)[:, :, tsl],
                    (xresB[J],) if l > 0 else (), tuple(xtB))
              P.dma("sp", cs[:, :], csscr[0][:, tsl], (csregB[J],), (csB,))
              P.dma("sp", sn[:, :], csscr[1][:, tsl], (csregB[J],), (snB,))
              MEMSET("pool", Qm[0][64:128, :, :], 0.0, tuple(QmB[0]))
              MEMSET("pool", Qm[1][0:64, :, :], 0.0, tuple(QmB[1]))
              norm_mod(J, 0, 0)
              if debug_stop == 1 and J == debug_tile:
                  raise _Stop()

              w2, w2B = get_group(l, J, 1)
              w1, w1B = get_group(l, J, 0)
              for c in range(4):
                  CP("pool", abuf[:, c, 0:30], ahalo[:, c, :], (ahB[c],), (abufB[c],))
                  pA, pAB = proj(w2, w2B, c)
                  pB_, pBB = proj(w1, w1B, c)
                  sg_, sgB = next_t32()
                  ACT(sg_[:, :], pA[:, :], AF.Sigmoid, (pAB,), (sgB,))
                  TT("dve", abuf[:, c, 30:30 + T], pB_[:, :], sg_[:, :], ALU.mult, (pBB, sgB), (abufB[c],))
              caw = lambda c, k: pv[:, PV_CAW + c * 31 + k: PV_CAW + c * 31 + k + 1]
              for k in range(31):
                  for c in range(4):
                      eng = "dve"
                      if k == 0:
                          TS(eng, acc[:, c, :], abuf[:, c, 0:T], caw(c, 0), pv[:, PV_CAB + c:PV_CAB + c + 1],
                             ALU.mult, ALU.add, (abufB[c], pvB), (accB[c],))
                      else:
                          STT(eng, acc[:, c, :], abuf[:, c, k:k + T], caw(c, k), acc[:, c, :], ALU.mult, ALU.add,
                              (abufB[c], pvB, accB[c]), (accB[c],))
              for c in range(4):
                  CP("pool", ahalo[:, c, :], abuf[:, c, T:T + 30], (abufB[c],), (ahB[c],))

              if debug_stop == 2 and J == debug_tile:
                  raise _Stop()
              wc, wcB = get_group(l, J, 3)
              for c in range(4):
                  p_, pB = proj(wc, wcB, c)
                  ACT(cgs[:, c, :], p_[:, :], AF.Copy, (pB,), (cgsB[c],))
              wx, wxB = get_group(l, J, 4)
              cbw = lambda c, k: pv[:, PV_CBW + c * 3 + k: PV_CBW + c * 3 + k + 1]
              for c in range(4):
                  CP("pool", ubuf[:, c, 0:2], uhalo[:, c, :], (uhB[c],), (ubufB[c],))
                  p_, pB = proj(wx, wxB, c)
                  TT("dve", ubuf[:, c, 2:2 + T], p_[:, :], cgs[:, c, :], ALU.mult, (pB, cgsB[c]), (ubufB[c],))
                  TSS("dve", cgs[:, c, :], ubuf[:, c, 0:T], cbw(c, 0), ALU.mult, (ubufB[c], pvB), (cgsB[c],))
                  for k in (1, 2):
                      STT("dve", cgs[:, c, :], ubuf[:, c, k:k + T], cbw(c, k), cgs[:, c, :], ALU.mult, ALU.add,
                          (ubufB[c], pvB, cgsB[c]), (cgsB[c],))
                  CP("pool", uhalo[:, c, :], ubuf[:, c, T:T + 2], (ubufB[c],), (uhB[c],))
              wg, wgB = get_group(l, J, 2)
              for c in range(4):
                  p_, pB = proj(wg, wgB, c)
                  TT("dve", bmix[:, c, :], p_[:, :], cgs[:, c, :], ALU.mult, (pB, cgsB[c]), (bmixB[c],))

              if debug_stop == 3 and J == debug_tile:
                  raise _Stop()
              for which, gids in (("q", (5, 6)), ("k", (7, 8))):
                  for gi, gid in enumerate(gids):
                      w, wB = get_group(l, J, gid)
                      for c in range(4):
                          ch = gi * 4 + c
                          p_, pB = proj(w, wB, c)
                          qi = ch % 2
                          ACT(qtmp[qi][:, :], p_[:, :], AF.Copy, (pB,), (qtmpB[qi],))
                          p2, p2B = next_ps()
                          MM(p2[:, :], perm[:, :], qtmp[qi][:, :], True, True, (constB, qtmpB[qi]), (p2B,))
                          t1, t1B = next_t32()
                          t2, t2B = next_t32()
                          TT("dve", t1[:, :], p_[:, :], cs[:, :], ALU.mult, (pB, csB), (t1B,))
                          TT("dve", t2[:, :], p2[:, :], sn[:, :], ALU.mult, (p2B, snB), (t2B,))
                          if which == "q":
                              TT("pool", Qm[0][0:64, ch, :], t1[0:64, :], t2[0:64, :], ALU.add, (t1B, t2B), (QmB[0][ch],))
                              TT("pool", Qm[1][64:128, ch, :], t1[64:128, :], t2[64:128, :], ALU.add, (t1B, t2B), (QmB[1][ch],))
                          else:
                              TT("pool", kt[:, ch, :], t1[:, :], t2[:, :], ALU.add, (t1B, t2B), (ktB[ch],))
              P.dma("pool", kscr.rearrange("c p t -> p c t")[:, :, tsl], kt[:, :, :], tuple(ktB), (kregB[J],))

              if debug_stop == 4 and J == debug_tile:
                  raise _Stop()
              MEMSET("pool", vt[:, :, :, 128:129], 1.0, tuple(vtB))
              for gi, gid in enumerate((9, 10)):
                  w, wB = get_group(l, J, gid)
                  for bq in range(4):
                      p_, pB = next_ps()
                      for kc in range(8):
                          MM(p_[:, :], ht[:, kc, bq * 128:(bq + 1) * 128], w[:, kc * 512:(kc + 1) * 512],
                             kc == 0, kc == 7, (wB, htB[kc]), (pB,))
                      src = p_[:, :].rearrange("p (h d) -> p h d", h=4)
                      dst = vt[:, gi * 4:gi * 4 + 4, bq, 0:128]
                      if bq % 2 == 0:
                          ACT(dst, src, AF.Copy, (pB,), (vtB[gi * 4 + bq],))
                      else:
                          CP("dve", dst, src, (pB,), (vtB[gi * 4 + bq],))
              P.dma("pool", vscr.rearrange("h p e -> p h e")[:, :, J * 516:(J + 1) * 516],
                    vt[:, :, :, :].rearrange("p h b e -> p h (b e)"), tuple(vtB), (vregB[J],))

              if debug_stop == 5 and J == debug_tile:
                  raise _Stop()
              pm, pmB = next_ps()
              pq, pqB = next_ps()
              for c in range(4):
                  a2, a2B = next_t32()
                  ACT(a2[:, :], acc[:, c, :], AF.Square, (accB[c],), (a2B,))
                  MM(pm[:, :], ones512[:, :], acc[:, c, :], c == 0, c == 3, (constB, accB[c]), (pmB,))
                  MM(pq[:, :], ones512[:, :], a2[:, :], c == 0, c == 3, (constB, a2B), (pqB,))
              mean, meanB = next_t32()
              ACT(mean[:, :], pm[:, :], AF.Copy, (pmB,), (meanB,))
              rsa, rsaB = next_t32()
              TT("dve", rsa[:, :], mean[:, :], mean[:, :], ALU.mult, (meanB,), (rsaB,))
              TT("dve", rsa[:, :], pq[:, :], rsa[:, :], ALU.subtract, (pqB, rsaB), (rsaB,))
              ACT(rsa[:, :], rsa[:, :], AF.Sqrt, (rsaB, constB), (rsaB,), bias=epsb[:, :])
              RECIP(rsa[:, :], rsa[:, :], (rsaB,), (rsaB,))
              for c in range(4):
                  TT("dve", acc[:, c, :], acc[:, c, :], mean[:, :], ALU.subtract, (accB[c], meanB), (accB[c],))
                  TT("dve", acc[:, c, :], acc[:, c, :], rsa[:, :], ALU.mult, (accB[c], rsaB), (accB[c],))
                  ACT(aact[:, c, :], acc[:, c, :], AF.Silu, (accB[c], pvB), (aactB[c],),
                      scale=pv[:, PV_LNG + c:PV_LNG + c + 1], bias=pv[:, PV_LNB + c:PV_LNB + c + 1])

              if debug_stop == 6 and J == debug_tile:
                  raise _Stop()
              pieces = [(h, comp, j) for h in range(8) for comp in range(2) for j in range(J + 1)]
              pstate = {"loaded": 0}

              def load_piece(i):
                  h, comp, j = pieces[i]
                  s = i % 4
                  P.dma("sp", kr[s][:, :], kscr[h][:, j * T:(j + 1) * T], (kregB[j],), (krB[s],))
                  P.dma("sp", vr[s][:, :], vscr[h][:, j * 516:(j + 1) * 516], (vregB[j],), (vrB[s],))

              pi = 0
              sidx = 0
              for h in range(8):
                  pend = None

                  def do_pv(st):
                      j, comp, bi, s, pti, r = st
                      i = 4 * j + bi
                      for b in range(max(r, 0), 4):
                          MM(ps[b][:, comp * 129:(comp + 1) * 129], pT[pti][:, b * 128:(b + 1) * 128],
                             vr[s][:, bi * 129:(bi + 1) * 129], i == 0, i == 4 * J + b,
                             (pTB[pti], vrB[s]), (psB[b],))

                  for comp in range(2):
                      for j in range(J + 1):
                          lim = min(pi + 3, len(pieces))
                          while pstate["loaded"] < lim:
                              load_piece(pstate["loaded"])
                              pstate["loaded"] += 1
                          s = pi % 4
                          pi += 1
                          for bi in range(4):
                              r = bi if j == J else -1
                              c0 = 128 * max(r, 0)
                              sb_i = 4 + sidx % 3
                              pti = sidx % 3
                              sidx += 1
                              MM(ps[sb_i][:, c0:T], kr[s][:, bi * 128:(bi + 1) * 128], Qm[comp][:, h, c0:T], True, True,
                                 (krB[s], QmB[comp][h]), (psB[sb_i],))
                              ACT(pT[pti][:, c0:T], ps[sb_i][:, c0:T], AF.Exp, (psB[sb_i],), (pTB[pti],), scale=0.125)
                              if r >= 0:
                                  TT("pool", pT[pti][:, c0:c0 + 128], pT[pti][:, c0:c0 + 128], tri[:, :], ALU.mult,
                                     (pTB[pti], constB), (pTB[pti],))
                              if pend is not None:
                                  do_pv(pend)
                              pend = (j, comp, bi, s, pti, r)
                  do_pv(pend)
                  for b in range(4):
                      ov = ps[b][:, 0:258].rearrange("p (c e) -> p c e", c=2)
                      RECIP(osm[:, 0:2], ov[:, :, 128], (psB[b],), (osmB,))
                      TT("dve", osm[:, 2:3], osm[:, 1:2], nlam, ALU.mult, (osmB, lamB), (osmB,))
                      TSS("dve", ofin[:, 0, :], ps[b][:, 0:128], osm[:, 0:1], ALU.mult, (psB[b], osmB), (ofinB[0],))
                      STT("dve", ofin[:, 1, :], ps[b][:, 129:257], osm[:, 2:3], ofin[:, 0, :], ALU.mult, ALU.add,
                          (psB[b], osmB, ofinB[0]), (ofinB[1],))
                      ACT(osq[:, :], ofin[:, 1, :], AF.Square, (ofinB[1],), (osqB,))
                      RSUM(osm[:, 3:4], osq[:, :], (osqB,), (osmB,))
                      ACT(osm[:, 4:5], osm[:, 3:4], AF.Sqrt, (osmB, constB), (osmB,), scale=1.0 / 128, bias=epsb[:, :])
                      RECIP(osm[:, 5:6], osm[:, 4:5], (osmB,), (osmB,))
                      TSS("dve", onb[b % 2][:, :], ofin[:, 1, :], osm[:, 5:6], ALU.mult, (ofinB[1], osmB), (onbB[b % 2],))
                      P.op("pe", lambda E, b=b: E.transpose(ptr[:, b * 128:(b + 1) * 128], onb[b % 2][:, :], ident[:, :]),
                           (onbB[b % 2], constB), (ptrB,))
                  ACT(oT[:, h, :], ptr[:, 0:T], AF.Copy, (ptrB, lamB), (oTB[h],), scale=sgc)

              if debug_stop == 7 and J == debug_tile:
                  raise _Stop()
              Prog.barrier(mgB + mgbB, arenaB)
              for br, (g0, gy, act_t, actB, nk, wdt) in enumerate((
                      (11, 17, aact, aactB, 4, 1024), (13, 18, bmix, bmixB, 4, 1024), (15, 19, oT, oTB, 8, 512))):
                  for hf in range(2):
                      wg_, wgB_ = get_group(l, J, g0 + hf)
                      wy, wyB = get_group(l, J, gy + (hf if br == 2 else 0))
                      for cc in range(4):
                          c = hf * 4 + cc
                          pG, pGB = proj(wg_, wgB_, cc)
                          coff = 0 if br == 2 else hf * 512
                          pY, pYB = proj(wy, wyB, cc, width=wdt, nkc=nk, rhs_t=act_t, rhsB=actB, coff=coff)
                          sg_, sgB = next_t32()
                          ACT(sg_[:, :], pG[:, :], AF.Sigmoid, (pGB,), (sgB,))
                          if br == 0:
                              TT("dve", mg[:, c, :], pY[:, :], sg_[:, :], ALU.mult, (pYB, sgB), (mgB[c],))
                          else:
                              TT("dve", sg_[:, :], pY[:, :], sg_[:, :], ALU.mult, (pYB, sgB), (sgB,))
                              if br == 1:
                                  TT("pool", mg[:, c, :], mg[:, c, :], sg_[:, :], ALU.add, (mgB[c], sgB), (mgB[c],))
                              else:
                                  TT("pool", mgb[:, c, :], mg[:, c, :], sg_[:, :], ALU.add, (mgB[c], sgB), (mgbB[c],))
              for hf in range(2):
                  w, wB = get_group(l, J, 21 + hf)
                  for cc in range(4):
                      c = hf * 4 + cc
                      p_, pB = proj(w, wB, cc, rhs_t=mgb, rhsB=mgbB)
                      STT("dve", xt[:, c, :], p_[:, :], ada[:, 16 + c:17 + c], xt[:, c, :], ALU.mult, ALU.add,
                          (pB, adaB, xtB[c]), (xtB[c],))

              if debug_stop == 8 and J == debug_tile:
                  raise _Stop()
              Prog.barrier(ftB, carryB)
              norm_mod(J, 8, 24)
              for g in range(8):
                  w, wB = get_group(l, J, 23 + g)
                  for cc in range(4):
                      p_, pB = proj(w, wB, cc)
                      r_, rB = next_t32()
                      ACT(r_[:, :], p_[:, :], AF.Relu, (pB,), (rB,))
                      TT("pool", ft[:, g * 4 + cc, :], r_[:, :], r_[:, :], ALU.mult, (rB,), (ftB[g * 4 + cc],))
              for hf in range(2):
                  banks = [next_ps() for _ in range(4)]
                  for q in range(4):
                      w, wB = get_group(l, J, 31 + hf * 4 + q)
                      for cc in range(4):
                          for kk in range(8):
                              MM(banks[cc][0][:, :], w[:, kk * 512 + cc * 128: kk * 512 + cc * 128 + 128],
                                 ft[:, q * 8 + kk, :], q == 0 and kk == 0, q == 3 and kk == 7,
                                 (wB, ftB[q * 8 + kk]), (banks[cc][1],))
                  for cc in range(4):
                      c = hf * 4 + cc
                      STT("dve", xt[:, c, :], banks[cc][0][:, :], ada[:, 40 + c:41 + c], xt[:, c, :], ALU.mult, ALU.add,
                          (banks[cc][1], adaB, xtB[c]), (xtB[c],))
              if last and final_norm:
                  rms_stats(J)
                  for c in range(NCH):
                      STT("dve", xt[:, c, :], xt[:, c, :], fg[:, c:c + 1], rstd[:, :], ALU.mult, ALU.mult,
                          (xtB[c], constB, rstdB), (xtB[c],))
              dst = outT if last else xres
              P.dma("pool", dst.rearrange("(c p) t -> p c t", p=128)[:, :, tsl], xt[:, :, :], tuple(xtB),
                    (xresB[J],) if not last else ())

    except _Stop:
        P.dma("pool", outT.rearrange("(c p) t -> p c t", p=128)[:, :, 0:T], xt[:, :, :], tuple(xtB), ())
    if debug_stop is None:
        assert wstate["used"] == len(use_seq)

    import contextlib
    with contextlib.ExitStack() as es:
        sems = {e: es.enter_context(nc.semaphore(f"s_{e}")) for e in Prog.ENGS}
        dsems = {e: [es.enter_context(nc.semaphore(f"d_{e}{i}")) for i in range(Prog.RING)] for e in ("sp", "pool")}
        block = es.enter_context(nc.Block())
        P.emit(nc, block, sems, dsems)
    counts = {e: len(P.lists[e]) for e in Prog.ENGS}
    return nc, counts


def _pc(v, nchunk):
    return np.ascontiguousarray(np.asarray(v, np.float32).reshape(nchunk, 128).T)


def make_consts():
    ident = np.eye(128, dtype=np.float32)
    perm = np.zeros((128, 128), np.float32)
    for m in range(128):
        if (m % 64) < 32:
            perm[m + 32, m] = -1.0
        else:
            perm[m - 32, m] = 1.0
    tri = (np.arange(128)[:, None] <= np.arange(128)[None, :]).astype(np.float32)
    inv_freq = (10000.0 ** (-np.arange(0, 64, 2, dtype=np.float32) / 64)).astype(np.float32)
    invf = (inv_freq[np.arange(128) % 32] / np.float32(2 * math.pi)).astype(np.float32)[:, None]
    return np.ascontiguousarray(np.concatenate([ident, perm, tri, invf], axis=1))


def make_pvec(inp, L):
    pv = np.zeros((L, 128, NPV), np.float32)
    lam = np.zeros((L, 128, 256), np.float32)
    for l in range(L):
        pv[l, :, PV_GMIX:PV_GMIX + 8] = _pc(inp["norm_mix_g"][l], 8)
        pv[l, :, PV_GMLP:PV_GMLP + 8] = _pc(inp["norm_mlp_g"][l], 8)
        caw = np.asarray(inp["conv_a_w"][l], np.float32)
        pv[l, :, PV_CAW:PV_CAW + 124] = caw.T.reshape(4, 128, 31).transpose(1, 0, 2).reshape(128, 124)
        pv[l, :, PV_CAB:PV_CAB + 4] = _pc(inp["conv_a_b"][l], 4)
        pv[l, :, PV_LNG:PV_LNG + 4] = _pc(inp["ln_a_g"][l], 4)
        pv[l, :, PV_LNB:PV_LNB + 4] = _pc(inp["ln_a_b"][l], 4)
        cbw = np.asarray(inp["conv_b_w"][l], np.float32)
        pv[l, :, PV_CBW:PV_CBW + 12] = cbw.T.reshape(4, 128, 3).transpose(1, 0, 2).reshape(128, 12)
        pv[l, :, PV_SUBG] = np.asarray(inp["subln_g"][l], np.float32)
        pv[l, :, PV_BADA:PV_BADA + 48] = _pc(inp["b_ada"][l], 48)
        row = np.concatenate([inp["lam_q1"][l], inp["lam_q2"][l], inp["lam_k1"][l], inp["lam_k2"][l]]).astype(np.float32)
        lam[l] = np.broadcast_to(row[None, :], (128, 256))
    return pv, lam


_CACHE = {}


def run_layers(inp, xT_list, S, L, final_norm, lambda_layer0=0, ncores=None):
    key = (S, L, final_norm, lambda_layer0)
    if key not in _CACHE:
        _CACHE[key] = build_program(S, L, final_norm, lambda_layer0)[0]
    nc = _CACHE[key]
    n = len(xT_list)
    consts = make_consts()
    pv, lam = make_pvec(inp, L)
    fgl = _pc(inp["final_g"], 8)
    f32 = lambda a: np.ascontiguousarray(np.asarray(a, np.float32))
    shared = {k: f32(inp[k]) for k in ("w_ada", "w_in", "w_a_out", "w_b_out", "w_c_out", "w_out", "w_ff1", "w_ff2")}
    in_maps = []
    for b in range(n):
        m = dict(shared)
        m["xT"] = xT_list[b]
        m["cT"] = _pc(np.asarray(inp["c"])[b], 8)
        m["posr"] = np.ascontiguousarray(np.broadcast_to(np.asarray(inp["positions"])[b].astype(np.int32)[None, :], (128, S)))
        m["consts"] = consts
        m["pvec"] = pv
        m["lamv"] = lam
        m["finalg"] = fgl
        in_maps.append(m)
    res = run_bass_kernel_spmd(nc, in_maps, core_ids=list(range(n)))
    return [np.asarray(r["outT"]) for r in res.results]


def kernel(**inputs):
    x = np.asarray(inputs["x"], np.float32)
    B, S, _ = x.shape
    L = int(np.asarray(inputs["w_in"]).shape[0])
    xT_list = [np.ascontiguousarray(x[b].T) for b in range(B)]
    outs = run_layers(inputs, xT_list, S, L, True)
    return np.ascontiguousarray(np.stack([o.T for o in outs], axis=0)).astype(np.float32)
```

```python
import math
import numpy as np
import concourse.bass as bass
import concourse.mybir as mybir
from concourse.bass_utils import run_bass_kernel_spmd

F32, BF16, I32 = mybir.dt.float32, mybir.dt.bfloat16, mybir.dt.int32
AF = mybir.ActivationFunctionType
ALU = mybir.AluOpType
AX = mybir.AxisListType

D = 1024
NCH = 8
T = 512
DFF = 4096
DIN = 8704
NG = 39
NPV = 8 + 8 + 124 + 4 + 4 + 4 + 12 + 1 + 48
PV_GMIX, PV_GMLP, PV_CAW, PV_CAB, PV_LNG, PV_LNB, PV_CBW, PV_SUBG, PV_BADA = 0, 8, 16, 140, 144, 148, 152, 164, 165


class Buf:
    __slots__ = ("name", "w", "rd", "rd_dma")

    def __init__(self, name):
        self.name = name
        self.w = None
        self.rd = {}
        self.rd_dma = []


class Ins:
    __slots__ = ("eng", "fn", "deps", "sig", "idx", "dma", "ev", "seq")


class Prog:
    ENGS = ("pe", "act", "dve", "pool", "sp")
    RING = 8

    def __init__(self):
        self.lists = {e: [] for e in self.ENGS}
        self.dma_n = {e: 0 for e in self.ENGS}
        self.dma_hist = {e: [] for e in self.ENGS}
        self.seq = 0

    def _add(self, eng, fn, reads, writes, dma):
        ins = Ins()
        ins.eng, ins.fn, ins.sig, ins.idx, ins.dma, ins.ev = eng, fn, False, 0, dma, None
        ins.seq = self.seq
        self.seq += 1
        deps = {}
        for b in reads:
            if b.w is not None:
                deps[id(b.w)] = b.w
            other = "dve" if eng == "pool" else ("pool" if eng == "dve" else None)
            if other is not None and other in b.rd:
                deps[id(b.rd[other])] = b.rd[other]
        for b in writes:
            if b.w is not None:
                deps[id(b.w)] = b.w
            for r in b.rd.values():
                deps[id(r)] = r
            for r in b.rd_dma:
                deps[id(r)] = r
        if dma:
            k = self.dma_n[eng]
            self.dma_n[eng] = k + 1
            ins.ev = (eng, k % self.RING, 16 * (k // self.RING + 1))
            hist = self.dma_hist[eng]
            if k >= self.RING:
                p = hist[k - self.RING]
                deps[id(p)] = p
            hist.append(ins)
        for b in reads:
            if dma:
                b.rd_dma.append(ins)
            else:
                b.rd[eng] = ins
        for b in writes:
            b.w = ins
            b.rd = {}
            b.rd_dma = []
        dl = []
        for d in deps.values():
            if d is ins:
                continue
            if (not d.dma) and (not dma) and d.eng == "pe" and eng == "pe":
                continue
            if not d.dma:
                d.sig = True
            dl.append(d)
        ins.deps = dl
        self.lists[eng].append(ins)
        return ins

    def op(self, eng, fn, reads=(), writes=()):
        return self._add(eng, fn, reads, writes, False)

    def dma(self, q, out, in_, reads=(), writes=()):
        return self._add(q, lambda E: E.dma_start(out=out, in_=in_), reads, writes, True)

    @staticmethod
    def barrier(new_bufs, old_bufs):
        rd = {}
        rd_dma = []
        for b in old_bufs:
            cands = list(b.rd.values())
            if b.w is not None:
                if b.w.dma:
                    rd_dma.append(b.w)
                else:
                    cands.append(b.w)
            for c in cands:
                if c.eng not in rd or rd[c.eng].seq < c.seq:
                    rd[c.eng] = c
            rd_dma.extend(b.rd_dma)
        for nb in new_bufs:
            nb.w = None
            nb.rd = dict(rd)
            nb.rd_dma = list(rd_dma)

    def emit(self, nc, block, sems, dsems):
        for e in self.ENGS:
            n = 0
            for ins in self.lists[e]:
                if ins.sig and not ins.dma:
                    n += 1
                    ins.idx = n

        def event(d):
            if d.dma:
                q, slot, val = d.ev
                return dsems[q][slot], val
            return sems[d.eng], d.idx

        def body_for(ename):
            def body(E):
                waited = {}
                for ins in self.lists[ename]:
                    for d in ins.deps:
                        sem, val = event(d)
                        if waited.get(id(sem), 0) < val:
                            E.wait_ge(sem, val)
                            waited[id(sem)] = val
                    r = ins.fn(E)
                    if ins.dma:
                        r.then_inc(dsems[ename][ins.ev[1]], 16)
                    elif ins.sig:
                        r.then_inc(sems[ename], 1)
                hist = self.dma_hist[ename]
                for ins in hist[-self.RING:]:
                    sem, val = event(ins)
                    if waited.get(id(sem), 0) < val:
                        E.wait_ge(sem, val)
                        waited[id(sem)] = val
            return body

        block.tensor(body_for("pe"))
        block.scalar(body_for("act"))
        block.vector(body_for("dve"))
        block.gpsimd(body_for("pool"))
        block.sync(body_for("sp"))


def group_table():
    g = {}
    for i in range(17):
        g[i] = ("w_in", 0, 8, i * 512, 512)
    g[17] = ("w_a_out", 0, 4, 0, 1024)
    g[18] = ("w_b_out", 0, 4, 0, 1024)
    g[19] = ("w_c_out", 0, 8, 0, 512)
    g[20] = ("w_c_out", 0, 8, 512, 512)
    g[21] = ("w_out", 0, 8, 0, 512)
    g[22] = ("w_out", 0, 8, 512, 512)
    for i in range(8):
        g[23 + i] = ("w_ff1", 0, 8, i * 512, 512)
    for hf in range(2):
        for q in range(4):
            g[31 + hf * 4 + q] = ("w_ff2", q * 1024, 8, hf * 512, 512)
    return g


TILE_SEQ = ([1, 0, 3, 4, 2, 5, 6, 7, 8, 9, 10] +
            [11, 17, 12, 17, 13, 18, 14, 18, 15, 19, 16, 20, 21, 22] +
            list(range(23, 31)) + list(range(31, 39)))


class _Stop(Exception):
    pass


USE_WCACHE = False
CAST_ENG = "act"


def build_program(S, L, final_norm=True, lambda_layer0=0, debug_stop=None, debug_tile=0):
    NT = S // T
    NTB = S // 128
    nc = bass.Bass("TRN2", target_bir_lowering=False)
    P = Prog()
    GT = group_table()

    def din(name, shape, dt=F32):
        return nc.dram_tensor(name, shape, dt, kind="ExternalInput").ap()

    xT = din("xT", [D, S])
    cT = din("cT", [128, NCH])
    posr = din("posr", [128, S], I32)
    consts = din("consts", [128, 3 * 128 + 1])
    pvec = din("pvec", [L, 128, NPV])
    lamv = din("lamv", [L, 128, 256])
    finalg = din("finalg", [128, NCH])
    wd = {
        "w_ada": din("w_ada", [L, D, 6 * D]),
        "w_in": din("w_in", [L, D, DIN]),
        "w_a_out": din("w_a_out", [L, 512, D]),
        "w_b_out": din("w_b_out", [L, 512, D]),
        "w_c_out": din("w_c_out", [L, D, D]),
        "w_out": din("w_out", [L, D, D]),
        "w_ff1": din("w_ff1", [L, D, DFF]),
        "w_ff2": din("w_ff2", [L, DFF, D]),
    }
    outT = nc.dram_tensor("outT", [D, S], F32, kind="ExternalOutput").ap()

    def dscr(name, shape, dt):
        return nc.dram_tensor(name, shape, dt, kind="Internal").ap()

    xres = dscr("xres", [D, S], F32)
    wsc = dscr("wsc", [L * NG, 128, 4096], BF16)
    kscr = dscr("kscr", [NCH, 128, S], BF16)
    vscr = dscr("vscr", [8, 128, NTB * 129], BF16)
    csscr = dscr("csscr", [2, 128, S], F32)

    off = [20 * 1024]

    def sb(name, shape, dt, at=None):
        esz = 2 if dt == BF16 else 4
        nbytes = esz * int(np.prod(shape[1:]))
        nbytes = (nbytes + 63) // 64 * 64
        if at is None:
            at = off[0]
            off[0] += nbytes
        t = nc.alloc_sbuf_tensor_at(name, list(shape), dt, offset=at)
        return t, at, nbytes

    wr = [sb(f"wr{i}", [128, 4096], BF16)[0] for i in range(4)]
    wrB = [Buf(f"wr{i}") for i in range(4)]
    ws = [sb(f"ws{i}", [128, 2048], F32)[0] for i in range(2)]
    wsB = [Buf(f"ws{i}") for i in range(2)]
    xt = sb("xt", [128, NCH, T], F32)[0]
    xtB = [Buf(f"xt{c}") for c in range(NCH)]
    ht = sb("ht", [128, NCH, T], BF16)[0]
    htB = [Buf(f"ht{c}") for c in range(NCH)]
    sq = sb("sq", [128, NCH, T], BF16)[0]
    sqB = [Buf(f"sq{c}") for c in range(NCH)]
    rstd = sb("rstd", [128, T], F32)[0]
    rstdB = Buf("rstd")
    NT32 = 4
    t32 = [sb(f"t32_{i}", [128, T], F32)[0] for i in range(NT32)]
    t32B = [Buf(f"t32_{i}") for i in range(NT32)]
    cs = sb("cs", [128, T], F32)[0]
    sn = sb("sn", [128, T], F32)[0]
    csB, snB = Buf("cs"), Buf("sn")
    pv = sb("pv", [128, NPV], F32)[0]
    pvB = Buf("pv")
    lamt = sb("lamt", [128, 256], F32)[0]
    lamp = sb("lamp", [128, 2, 64], F32)[0]
    lams = sb("lams", [128, 8], F32)[0]
    lamB = Buf("lam")
    ada = sb("ada", [128, 48], F32)[0]
    gm = sb("gm", [128, 16], F32)[0]
    adaB = Buf("ada")
    cact = sb("cact", [128, NCH], F32)[0]
    cactB = Buf("cact")
    cst = sb("cst", [128, 3 * 128 + 1], F32)[0]
    cstB = Buf("cst")
    onesD = sb("onesD", [128, 128], BF16)[0]
    ones512 = sb("ones512", [128, 128], F32)[0]
    ident = sb("ident", [128, 128], BF16)[0]
    perm = sb("perm", [128, 128], BF16)[0]
    tri = sb("tri", [128, 128], BF16)[0]
    epsb = sb("epsb", [128, 1], F32)[0]
    negpi = sb("negpi", [128, 1], F32)[0]
    fg = sb("fg", [128, NCH], F32)[0]
    constB = Buf("const")
    ahalo = sb("ahalo", [128, 4, 30], F32)[0]
    uhalo = sb("uhalo", [128, 4, 2], F32)[0]
    ahB = [Buf(f"ah{c}") for c in range(4)]
    uhB = [Buf(f"uh{c}") for c in range(4)]
    pint = sb("pint", [128, T], I32)[0]
    pintB = Buf("pint")
    kr = [sb(f"kr{i}", [128, T], BF16)[0] for i in range(4)]
    vr = [sb(f"vr{i}", [128, 4 * 129], BF16)[0] for i in range(4)]
    krB = [Buf(f"kr{i}") for i in range(4)]
    vrB = [Buf(f"vr{i}") for i in range(4)]
    pT = [sb(f"pT{i}", [128, T], BF16)[0] for i in range(3)]
    pTB = [Buf(f"pT{i}") for i in range(3)]
    ofin = sb("ofin", [128, 2, 128], F32)[0]
    ofinB = [Buf("ofin0"), Buf("ofin1")]
    osq = sb("osq", [128, 128], F32)[0]
    osqB = Buf("osq")
    osm = sb("osm", [128, 8], F32)[0]
    osmB = Buf("osm")
    onb = [sb(f"onb{i}", [128, 128], BF16)[0] for i in range(2)]
    onbB = [Buf("onb0"), Buf("onb1")]
    carry0 = off[0]
    Qm = [sb(f"Qm{i}", [128, NCH, T], BF16)[0] for i in range(2)]
    QmB = [[Buf(f"Qm{i}_{c}") for c in range(NCH)] for i in range(2)]
    oT = sb("oT", [128, 8, T], BF16)[0]
    oTB = [Buf(f"oT{h}") for h in range(8)]
    aact = sb("aact", [128, 4, T], BF16)[0]
    aactB = [Buf(f"aact{c}") for c in range(4)]
    bmix = sb("bmix", [128, 4, T], BF16)[0]
    bmixB = [Buf(f"bmix{c}") for c in range(4)]
    carry_end = off[0]
    ft = sb("ft", [128, 32, T], BF16, at=carry0)[0]
    ftB = [Buf(f"ft{c}") for c in range(32)]
    assert carry_end - carry0 == 32 * T * 2, (carry_end - carry0)
    carryB = [b for q in QmB for b in q] + oTB + aactB + bmixB
    arena0 = off[0]
    abuf = sb("abuf", [128, 4, 30 + T], F32)[0]
    abufB = [Buf(f"abuf{c}") for c in range(4)]
    acc = sb("acc", [128, 4, T], F32)[0]
    accB = [Buf(f"acc{c}") for c in range(4)]
    cgs = sb("cgs", [128, 4, T], F32)[0]
    cgsB = [Buf(f"cgs{c}") for c in range(4)]
    ubuf = sb("ubuf", [128, 4, 2 + T], F32)[0]
    ubufB = [Buf(f"ubuf{c}") for c in range(4)]
    qtmp = [sb(f"qtmp{i}", [128, T], BF16)[0] for i in range(2)]
    qtmpB = [Buf("qtmp0"), Buf("qtmp1")]
    kt = sb("kt", [128, NCH, T], BF16)[0]
    ktB = [Buf(f"kt{c}") for c in range(NCH)]
    vt = sb("vt", [128, 8, 4, 129], BF16)[0]
    vtB = [Buf(f"vt{i}") for i in range(8)]
    arena_end = off[0]
    mg = sb("mg", [128, NCH, T], F32, at=arena0)[0]
    mgb = sb("mgb", [128, NCH, T], BF16, at=arena0 + NCH * T * 4)[0]
    mgB = [Buf(f"mg{c}") for c in range(NCH)]
    mgbB = [Buf(f"mgb{c}") for c in range(NCH)]
    assert arena_end - arena0 >= NCH * T * 6
    arenaB = abufB + accB + cgsB + ubufB + qtmpB + ktB + vtB
    assert off[0] <= 224 * 1024, off[0]

    ps = [nc.alloc_psum_tensor(f"ps{i}", [128, 512], F32) for i in range(7)]
    psB = [Buf(f"ps{i}") for i in range(7)]
    ptr = nc.alloc_psum_tensor("ptr", [128, 1024], BF16)
    ptrB = Buf("ptr")
    ps_rr = [0]

    def next_ps():
        i = ps_rr[0] % 7
        ps_rr[0] += 1
        return ps[i], psB[i]

    t32_rr = [0]

    def next_t32():
        i = t32_rr[0] % NT32
        t32_rr[0] += 1
        return t32[i], t32B[i]

    xresB = [Buf(f"xres{j}") for j in range(NT)]
    kregB = [Buf(f"kreg{j}") for j in range(NT)]
    vregB = [Buf(f"vreg{j}") for j in range(NT)]
    csregB = [Buf(f"csreg{j}") for j in range(NT)]
    wscB = {}

    def MM(out, lhsT, rhs, start, stop, rd, wr_):
        P.op("pe", lambda E: E.matmul(out, lhsT, rhs, start=start, stop=stop), rd, wr_)

    def ACT(out, in_, func, rd, wr_, **kw):
        P.op("act", lambda E: E.activation(out=out, in_=in_, func=func, **kw), rd, wr_)

    def TT(eng, out, in0, in1, op, rd, wr_):
        P.op(eng, lambda E: E.tensor_tensor(out=out, in0=in0, in1=in1, op=op), rd, wr_)

    def TS(eng, out, in0, s1, s2, op0, op1, rd, wr_):
        P.op(eng, lambda E: E.tensor_scalar(out=out, in0=in0, scalar1=s1, scalar2=s2, op0=op0, op1=op1), rd, wr_)

    def TSS(eng, out, in_, s, op, rd, wr_):
        P.op(eng, lambda E: E.tensor_single_scalar(out=out, in_=in_, scalar=s, op=op), rd, wr_)

    def STT(eng, out, in0, scalar, in1, op0, op1, rd, wr_):
        P.op(eng, lambda E: E.scalar_tensor_tensor(out=out, in0=in0, scalar=scalar, in1=in1, op0=op0, op1=op1), rd, wr_)

    def CP(eng, out, in_, rd, wr_):
        if eng == "act":
            P.op(eng, lambda E: E.activation(out=out, in_=in_, func=AF.Copy), rd, wr_)
        else:
            P.op(eng, lambda E: E.tensor_copy(out=out, in_=in_), rd, wr_)

    def RECIP(out, in_, rd, wr_):
        P.op("dve", lambda E: E.reciprocal(out=out, in_=in_), rd, wr_)

    def RSUM(out, in_, rd, wr_):
        P.op("dve", lambda E: E.reduce_sum(out=out, in_=in_, axis=AX.X), rd, wr_)

    def MEMSET(eng, ap, val, wr_):
        P.op(eng, lambda E: E.memset(ap, val), (), wr_)

    use_seq = []
    for l in range(L):
        for J in range(NT):
            for gid in TILE_SEQ:
                use_seq.append((l, J, gid))
    wstate = {"loaded": 0, "used": 0, "stage": 0}
    LOOKAHEAD = 2

    def src_piece(l, gid, hh):
        name, row0, nkc, col0, width = GT[gid]
        w = wd[name][l]
        hk = nkc // 2
        r0 = row0 + hh * hk * 128
        return w[r0:r0 + hk * 128, col0:col0 + width].rearrange("(k p) n -> p k n", p=128), hk, width

    def record_load(i):
        l, J, gid = use_seq[i]
        slot = i % 4
        key = (l, gid)
        if (key not in wscB) or not USE_WCACHE:
            wscB[key] = Buf(f"wsc{l}_{gid}")
            for hh in range(2):
                src, hk, width = src_piece(l, gid, hh)
                si = wstate["stage"] % 2
                wstate["stage"] += 1
                P.dma("sp", ws[si][:, :].rearrange("p (k n) -> p k n", k=hk), src, (), (wsB[si],))
                CP("pool", wr[slot][:, hh * 2048:(hh + 1) * 2048], ws[si][:, :], (wsB[si],), (wrB[slot],))
            if USE_WCACHE:
                P.dma("act", wsc[l * NG + gid], wr[slot][:, :], (wrB[slot],), (wscB[key],))
        else:
            for qq in range(4):
                P.dma("sp", wr[slot][:, qq * 1024:(qq + 1) * 1024], wsc[l * NG + gid][:, qq * 1024:(qq + 1) * 1024],
                      (wscB[key],), (wrB[slot],))

    def get_group(l, J, gid):
        i = wstate["used"]
        assert use_seq[i] == (l, J, gid), (use_seq[i], (l, J, gid))
        lim = min(i + 1 + LOOKAHEAD, len(use_seq))
        while wstate["loaded"] < lim:
            record_load(wstate["loaded"])
            wstate["loaded"] += 1
        wstate["used"] = i + 1
        slot = i % 4
        return wr[slot], wrB[slot]

    P.dma("sp", cst[:, :], consts, (), (cstB,))
    CP("dve", ident[:, :], cst[:, 0:128], (cstB,), (constB,))
    CP("dve", perm[:, :], cst[:, 128:256], (cstB,), (constB,))
    CP("dve", tri[:, :], cst[:, 256:384], (cstB,), (constB,))
    MEMSET("pool", onesD[:, :], 1.0 / D, (constB,))
    MEMSET("pool", ones512[:, :], 1.0 / 512, (constB,))
    MEMSET("pool", epsb[:, :], 1e-6, (constB,))
    MEMSET("pool", negpi[:, :], -math.pi, (constB,))
    P.dma("sp", fg[:, :], finalg, (), (constB,))
    P.dma("sp", cact[:, :], cT, (), (cactB,))
    ACT(cact[:, :], cact[:, :], AF.Silu, (cactB,), (cactB,))
    invf = cst[:, 384:385]
    for j in range(NT):
        P.dma("sp", pint[:, :], posr[:, j * T:(j + 1) * T], (), (pintB,))
        yb, ybB = next_t32()
        CP("dve", yb[:, :], pint[:, :], (pintB,), (ybB,))
        for which, addc in ((0, 0.75), (1, 0.5)):
            y2, y2B = next_t32()
            TS("dve", y2[:, :], yb[:, :], invf, addc, ALU.mult, ALU.add, (ybB, cstB), (y2B,))
            CP("dve", pint[:, :], y2[:, :], (y2B,), (pintB,))
            y3, y3B = next_t32()
            CP("dve", y3[:, :], pint[:, :], (pintB,), (y3B,))
            TT("dve", y2[:, :], y2[:, :], y3[:, :], ALU.subtract, (y2B, y3B), (y2B,))
            TSS("dve", y3[:, :], y2[:, :], 0.0, ALU.is_lt, (y2B,), (y3B,))
            TT("dve", y2[:, :], y2[:, :], y3[:, :], ALU.add, (y2B, y3B), (y2B,))
            ACT(y2[:, :], y2[:, :], AF.Sin, (y2B, constB), (y2B,), scale=2 * math.pi, bias=negpi[:, :])
            P.dma("sp", csscr[which][:, j * T:(j + 1) * T], y2[:, :], (y2B,), (csregB[j],))
    for c in range(4):
        MEMSET("pool", ahalo[:, c, :], 0.0, (ahB[c],))
        MEMSET("pool", uhalo[:, c, :], 0.0, (uhB[c],))

    def rms_stats(J):
        for c in range(NCH):
            ACT(sq[:, c, :], xt[:, c, :], AF.Square, (xtB[c],), (sqB[c],))
        pn, pnB = next_ps()
        for c in range(NCH):
            MM(pn[:, :], onesD[:, :], sq[:, c, :], c == 0, c == NCH - 1, (constB, sqB[c]), (pnB,))
        ACT(rstd[:, :], pn[:, :], AF.Sqrt, (pnB, constB), (rstdB,), bias=epsb[:, :])
        RECIP(rstd[:, :], rstd[:, :], (rstdB,), (rstdB,))

    def norm_mod(J, gcol, shcol):
        rms_stats(J)
        for c in range(NCH):
            tb, tbB = next_t32()
            STT("dve", tb[:, :], xt[:, c, :], gm[:, gcol + c:gcol + c + 1], rstd[:, :], ALU.mult, ALU.mult,
                (xtB[c], adaB, rstdB), (tbB,))
            ACT(ht[:, c, :], tb[:, :], AF.Identity, (tbB, adaB), (htB[c],), bias=ada[:, shcol + c:shcol + c + 1])

    def proj(w, wB, cc, width=512, nkc=8, rhs_t=None, rhsB=None, coff=0):
        p_, pB = next_ps()
        rhs_t = ht if rhs_t is None else rhs_t
        rhsB = htB if rhsB is None else rhsB
        for kc in range(nkc):
            MM(p_[:, :], w[:, kc * width + coff + cc * 128: kc * width + coff + cc * 128 + 128], rhs_t[:, kc, :],
               kc == 0, kc == nkc - 1, (wB, rhsB[kc]), (pB,))
        return p_, pB

    try:
      for l in range(L):
          lam_init = 0.8 - 0.6 * math.exp(-0.3 * (l + lambda_layer0))
          P.dma("sp", pv[:, :], pvec[l], (), (pvB,))
          P.dma("sp", lamt[:, :], lamv[l], (), (lamB,))
          TT("dve", lamp[:, :, :], lamt[:, 0:128].rearrange("p (a b) -> p a b", a=2),
             lamt[:, 128:256].rearrange("p (a b) -> p a b", a=2), ALU.mult, (lamB,), (lamB,))
          RSUM(lams[:, 0:2], lamp[:, :, :], (lamB,), (lamB,))
          ACT(lams[:, 0:2], lams[:, 0:2], AF.Exp, (lamB,), (lamB,))
          TT("dve", lams[:, 2:3], lams[:, 0:1], lams[:, 1:2], ALU.subtract, (lamB,), (lamB,))
          TS("dve", lams[:, 3:4], lams[:, 2:3], lam_init, -1.0, ALU.add, ALU.mult, (lamB,), (lamB,))
          TSS("dve", lams[:, 4:5], pv[:, PV_SUBG:PV_SUBG + 1], 1.0 - lam_init, ALU.mult, (pvB, lamB), (lamB,))
          nlam = lams[:, 3:4]
          sgc = lams[:, 4:5]
          pa, paB = next_ps()
          for pc in range(24):
              si = wstate["stage"] % 2
              wstate["stage"] += 1
              src = wd["w_ada"][l][:, pc * 256:(pc + 1) * 256].rearrange("(k p) n -> p k n", p=128)
              P.dma("sp", ws[si][:, :].rearrange("p (k n) -> p k n", k=8), src, (), (wsB[si],))
              for cc in range(2):
                  j = pc * 2 + cc
                  for kc in range(8):
                      MM(pa[:, j:j + 1], ws[si][:, kc * 256 + cc * 128: kc * 256 + cc * 128 + 128], cact[:, kc:kc + 1],
                         kc == 0, kc == 7, (wsB[si], cactB), (paB,))
          TT("dve", ada[:, :], pa[:, 0:48], pv[:, PV_BADA:PV_BADA + 48], ALU.add, (paB, pvB), (adaB,))
          STT("dve", gm[:, 0:8], ada[:, 8:16], 1.0, pv[:, PV_GMIX:PV_GMIX + 8], ALU.add, ALU.mult, (adaB, pvB), (adaB,))
          STT("dve", gm[:, 8:16], ada[:, 32:40], 1.0, pv[:, PV_GMLP:PV_GMLP + 8], ALU.add, ALU.mult, (adaB, pvB), (adaB,))
          for c in range(4):
              MEMSET("pool", ahalo[:, c, :], 0.0, (ahB[c],))
              MEMSET("pool", uhalo[:, c, :], 0.0, (uhB[c],))

          xsrc = xT if l == 0 else xres
          last = (l == L - 1)

          for J in range(NT):
              tsl = slice(J * T, (J + 1) * T)
              Prog.barrier(arenaB, mgB + mgbB)
              Prog.barrier(carryB, ftB)
              P.dma("sp", xt[:, :, :], xsrc.rearrange("(c p) t -> p c t", p=128)[:, :, tsl],
                    (xresB[J],) if l > 0 else (), tuple(xtB))
              P.dma("sp", cs[:, :], csscr[0][:, tsl], (csregB[J],), (csB,))
              P.dma("sp", sn[:, :], csscr[1][:, tsl], (csregB[J],), (snB,))
              MEMSET("pool", Qm[0][64:128, :, :], 0.0, tuple(QmB[0]))
              MEMSET("pool", Qm[1][0:64, :, :], 0.0, tuple(QmB[1]))
              norm_mod(J, 0, 0)
              if debug_stop == 1 and J == debug_tile:
                  raise _Stop()

              w2, w2B = get_group(l, J, 1)
              w1, w1B = get_group(l, J, 0)
              for c in range(4):
                  CP("pool", abuf[:, c, 0:30], ahalo[:, c, :], (ahB[c],), (abufB[c],))
                  pA, pAB = proj(w2, w2B, c)
                  pB_, pBB = proj(w1, w1B, c)
                  sg_, sgB = next_t32()
                  ACT(sg_[:, :], pA[:, :], AF.Sigmoid, (pAB,), (sgB,))
                  TT("dve", abuf[:, c, 30:30 + T], pB_[:, :], sg_[:, :], ALU.mult, (pBB, sgB), (abufB[c],))
              if debug_stop == 11 and J == debug_tile:
                  raise _Stop()
              caw = lambda c, k: pv[:, PV_CAW + c * 31 + k: PV_CAW + c * 31 + k + 1]
              for k in range(31):
                  for c in range(4):
                      eng = "dve"
                      if k == 0:
                          TS(eng, acc[:, c, :], abuf[:, c, 0:T], caw(c, 0), pv[:, PV_CAB + c:PV_CAB + c + 1],
                             ALU.mult, ALU.add, (abufB[c], pvB), (accB[c],))
                      else:
                          STT(eng, acc[:, c, :], abuf[:, c, k:k + T], caw(c, k), acc[:, c, :], ALU.mult, ALU.add,
                              (abufB[c], pvB, accB[c]), (accB[c],))
              for c in range(4):
                  CP("pool", ahalo[:, c, :], abuf[:, c, T:T + 30], (abufB[c], accB[c]), (ahB[c],))

              if debug_stop == 2 and J == debug_tile:
                  raise _Stop()
              wc, wcB = get_group(l, J, 3)
              for c in range(4):
                  p_, pB = proj(wc, wcB, c)
                  ACT(cgs[:, c, :], p_[:, :], AF.Copy, (pB,), (cgsB[c],))
              wx, wxB = get_group(l, J, 4)
              cbw = lambda c, k: pv[:, PV_CBW + c * 3 + k: PV_CBW + c * 3 + k + 1]
              for c in range(4):
                  CP("pool", ubuf[:, c, 0:2], uhalo[:, c, :], (uhB[c],), (ubufB[c],))
                  p_, pB = proj(wx, wxB, c)
                  TT("dve", ubuf[:, c, 2:2 + T], p_[:, :], cgs[:, c, :], ALU.mult, (pB, cgsB[c]), (ubufB[c],))
                  TSS("dve", cgs[:, c, :], ubuf[:, c, 0:T], cbw(c, 0), ALU.mult, (ubufB[c], pvB), (cgsB[c],))
                  for k in (1, 2):
                      STT("dve", cgs[:, c, :], ubuf[:, c, k:k + T], cbw(c, k), cgs[:, c, :], ALU.mult, ALU.add,
                          (ubufB[c], pvB, cgsB[c]), (cgsB[c],))
                  CP("pool", uhalo[:, c, :], ubuf[:, c, T:T + 2], (ubufB[c],), (uhB[c],))
              wg, wgB = get_group(l, J, 2)
              for c in range(4):
                  p_, pB = proj(wg, wgB, c)
                  TT("dve", bmix[:, c, :], p_[:, :], cgs[:, c, :], ALU.mult, (pB, cgsB[c]), (bmixB[c],))

              if debug_stop == 3 and J == debug_tile:
                  raise _Stop()
              for which, gids in (("q", (5, 6)), ("k", (7, 8))):
                  for gi, gid in enumerate(gids):
                      w, wB = get_group(l, J, gid)
                      for c in range(4):
                          ch = gi * 4 + c
                          p_, pB = proj(w, wB, c)
                          qi = ch % 2
                          ACT(qtmp[qi][:, :], p_[:, :], AF.Copy, (pB,), (qtmpB[qi],))
                          p2, p2B = next_ps()
                          MM(p2[:, :], perm[:, :], qtmp[qi][:, :], True, True, (constB, qtmpB[qi]), (p2B,))
                          t1, t1B = next_t32()
                          t2, t2B = next_t32()
                          TT("dve", t1[:, :], p_[:, :], cs[:, :], ALU.mult, (pB, csB), (t1B,))
                          TT("dve", t2[:, :], p2[:, :], sn[:, :], ALU.mult, (p2B, snB), (t2B,))
                          if which == "q":
                              TT("dve", Qm[0][0:64, ch, :], t1[0:64, :], t2[0:64, :], ALU.add, (t1B, t2B), (QmB[0][ch],))
                              TT("dve", Qm[1][64:128, ch, :], t1[64:128, :], t2[64:128, :], ALU.add, (t1B, t2B), (QmB[1][ch],))
                          else:
                              TT("dve", kt[:, ch, :], t1[:, :], t2[:, :], ALU.add, (t1B, t2B), (ktB[ch],))
              P.dma("act", kscr.rearrange("c p t -> p c t")[:, :, tsl], kt[:, :, :], tuple(ktB), (kregB[J],))

              if debug_stop == 4 and J == debug_tile:
                  raise _Stop()
              MEMSET("pool", vt[:, :, :, 128:129], 1.0, tuple(vtB))
              for gi, gid in enumerate((9, 10)):
                  w, wB = get_group(l, J, gid)
                  for bq in range(4):
                      p_, pB = next_ps()
                      for kc in range(8):
                          MM(p_[:, :], ht[:, kc, bq * 128:(bq + 1) * 128], w[:, kc * 512:(kc + 1) * 512],
                             kc == 0, kc == 7, (wB, htB[kc]), (pB,))
                      src = p_[:, :].rearrange("p (h d) -> p h d", h=4)
                      dst = vt[:, gi * 4:gi * 4 + 4, bq, 0:128]
                      if bq % 2 == 0:
                          ACT(dst, src, AF.Copy, (pB,), (vtB[gi * 4 + bq],))
                      else:
                          CP("dve", dst, src, (pB,), (vtB[gi * 4 + bq],))
              P.dma("act", vscr.rearrange("h p e -> p h e")[:, :, J * 516:(J + 1) * 516],
                    vt[:, :, :, :].rearrange("p h b e -> p h (b e)"), tuple(vtB), (vregB[J],))

              if debug_stop == 5 and J == debug_tile:
                  raise _Stop()
              pm, pmB = next_ps()
              pq, pqB = next_ps()
              for c in range(4):
                  a2, a2B = next_t32()
                  ACT(a2[:, :], acc[:, c, :], AF.Square, (accB[c],), (a2B,))
                  MM(pm[:, :], ones512[:, :], acc[:, c, :], c == 0, c == 3, (constB, accB[c]), (pmB,))
                  MM(pq[:, :], ones512[:, :], a2[:, :], c == 0, c == 3, (constB, a2B), (pqB,))
              mean, meanB = next_t32()
              ACT(mean[:, :], pm[:, :], AF.Copy, (pmB,), (meanB,))
              rsa, rsaB = next_t32()
              TT("dve", rsa[:, :], mean[:, :], mean[:, :], ALU.mult, (meanB,), (rsaB,))
              TT("dve", rsa[:, :], pq[:, :], rsa[:, :], ALU.subtract, (pqB, rsaB), (rsaB,))
              ACT(rsa[:, :], rsa[:, :], AF.Sqrt, (rsaB, constB), (rsaB,), bias=epsb[:, :])
              RECIP(rsa[:, :], rsa[:, :], (rsaB,), (rsaB,))
              for c in range(4):
                  TT("dve", acc[:, c, :], acc[:, c, :], mean[:, :], ALU.subtract, (accB[c], meanB), (accB[c],))
                  TT("dve", acc[:, c, :], acc[:, c, :], rsa[:, :], ALU.mult, (accB[c], rsaB), (accB[c],))
                  ACT(aact[:, c, :], acc[:, c, :], AF.Silu, (accB[c], pvB), (aactB[c],),
                      scale=pv[:, PV_LNG + c:PV_LNG + c + 1], bias=pv[:, PV_LNB + c:PV_LNB + c + 1])

              if debug_stop == 6 and J == debug_tile:
                  raise _Stop()
              pieces = [(h, comp, j) for h in range(8) for comp in range(2) for j in range(J + 1)]
              pstate = {"loaded": 0}

              def load_piece(i):
                  h, comp, j = pieces[i]
                  s = i % 4
                  P.dma("sp", kr[s][:, :], kscr[h][:, j * T:(j + 1) * T], (kregB[j],), (krB[s],))
                  P.dma("sp", vr[s][:, :], vscr[h][:, j * 516:(j + 1) * 516], (vregB[j],), (vrB[s],))

              pi = 0
              sidx = 0
              for h in range(8):
                  pend = None

                  def do_pv(st):
                      j, comp, bi, s, pti, r = st
                      i = 4 * j + bi
                      for b in range(max(r, 0), 4):
                          MM(ps[b][:, comp * 129:(comp + 1) * 129], pT[pti][:, b * 128:(b + 1) * 128],
                             vr[s][:, bi * 129:(bi + 1) * 129], i == 0, i == 4 * J + b,
                             (pTB[pti], vrB[s]), (psB[b],))

                  for comp in range(2):
                      for j in range(J + 1):
                          lim = min(pi + 3, len(pieces))
                          while pstate["loaded"] < lim:
                              load_piece(pstate["loaded"])
                              pstate["loaded"] += 1
                          s = pi % 4
                          pi += 1
                          for bi in range(4):
                              r = bi if j == J else -1
                              c0 = 128 * max(r, 0)
                              sb_i = 4 + sidx % 3
                              pti = sidx % 3
                              sidx += 1
                              MM(ps[sb_i][:, c0:T], kr[s][:, bi * 128:(bi + 1) * 128], Qm[comp][:, h, c0:T], True, True,
                                 (krB[s], QmB[comp][h]), (psB[sb_i],))
                              ACT(pT[pti][:, c0:T], ps[sb_i][:, c0:T], AF.Exp, (psB[sb_i],), (pTB[pti],), scale=0.125)
                              if r >= 0:
                                  TT("dve", pT[pti][:, c0:c0 + 128], pT[pti][:, c0:c0 + 128], tri[:, :], ALU.mult,
                                     (pTB[pti], constB), (pTB[pti],))
                              if pend is not None:
                                  do_pv(pend)
                              pend = (j, comp, bi, s, pti, r)
                  do_pv(pend)
                  for b in range(4):
                      ov = ps[b][:, 0:258].rearrange("p (c e) -> p c e", c=2)
                      RECIP(osm[:, 0:2], ov[:, :, 128], (psB[b],), (osmB,))
                      TT("dve", osm[:, 2:3], osm[:, 1:2], nlam, ALU.mult, (osmB, lamB), (osmB,))
                      TSS("dve", ofin[:, 0, :], ps[b][:, 0:128], osm[:, 0:1], ALU.mult, (psB[b], osmB), (ofinB[0],))
                      STT("dve", ofin[:, 1, :], ps[b][:, 129:257], osm[:, 2:3], ofin[:, 0, :], ALU.mult, ALU.add,
                          (psB[b], osmB, ofinB[0]), (ofinB[1],))
                      ACT(osq[:, :], ofin[:, 1, :], AF.Square, (ofinB[1],), (osqB,))
                      RSUM(osm[:, 3:4], osq[:, :], (osqB,), (osmB,))
                      ACT(osm[:, 4:5], osm[:, 3:4], AF.Sqrt, (osmB, constB), (osmB,), scale=1.0 / 128, bias=epsb[:, :])
                      RECIP(osm[:, 5:6], osm[:, 4:5], (osmB,), (osmB,))
                      TSS("dve", onb[b % 2][:, :], ofin[:, 1, :], osm[:, 5:6], ALU.mult, (ofinB[1], osmB), (onbB[b % 2],))
                      P.op("pe", lambda E, b=b: E.transpose(ptr[:, b * 128:(b + 1) * 128], onb[b % 2][:, :], ident[:, :]),
                           (onbB[b % 2], constB), (ptrB,))
                  ACT(oT[:, h, :], ptr[:, 0:T], AF.Copy, (ptrB, lamB), (oTB[h],), scale=sgc)

              if debug_stop == 7 and J == debug_tile:
                  raise _Stop()
              Prog.barrier(mgB + mgbB, arenaB)
              for br, (g0, gy, act_t, actB, nk, wdt) in enumerate((
                      (11, 17, aact, aactB, 4, 1024), (13, 18, bmix, bmixB, 4, 1024), (15, 19, oT, oTB, 8, 512))):
                  for hf in range(2):
                      wg_, wgB_ = get_group(l, J, g0 + hf)
                      wy, wyB = get_group(l, J, gy + (hf if br == 2 else 0))
                      for cc in range(4):
                          c = hf * 4 + cc
                          pG, pGB = proj(wg_, wgB_, cc)
                          coff = 0 if br == 2 else hf * 512
                          pY, pYB = proj(wy, wyB, cc, width=wdt, nkc=nk, rhs_t=act_t, rhsB=actB, coff=coff)
                          sg_, sgB = next_t32()
                          ACT(sg_[:, :], pG[:, :], AF.Sigmoid, (pGB,), (sgB,))
                          if br == 0:
                              TT("dve", mg[:, c, :], pY[:, :], sg_[:, :], ALU.mult, (pYB, sgB), (mgB[c],))
                          else:
                              TT("dve", sg_[:, :], pY[:, :], sg_[:, :], ALU.mult, (pYB, sgB), (sgB,))
                              if br == 1:
                                  TT("dve", mg[:, c, :], mg[:, c, :], sg_[:, :], ALU.add, (mgB[c], sgB), (mgB[c],))
                              else:
                                  TT("dve", mgb[:, c, :], mg[:, c, :], sg_[:, :], ALU.add, (mgB[c], sgB), (mgbB[c],))
              for hf in range(2):
                  w, wB = get_group(l, J, 21 + hf)
                  for cc in range(4):
                      c = hf * 4 + cc
                      p_, pB = proj(w, wB, cc, rhs_t=mgb, rhsB=mgbB)
                      STT("dve", xt[:, c, :], p_[:, :], ada[:, 16 + c:17 + c], xt[:, c, :], ALU.mult, ALU.add,
                          (pB, adaB, xtB[c]), (xtB[c],))

              if debug_stop == 8 and J == debug_tile:
                  raise _Stop()
              Prog.barrier(ftB, carryB)
              norm_mod(J, 8, 24)
              for g in range(8):
                  w, wB = get_group(l, J, 23 + g)
                  for cc in range(4):
                      p_, pB = proj(w, wB, cc)
                      r_, rB = next_t32()
                      ACT(r_[:, :], p_[:, :], AF.Relu, (pB,), (rB,))
                      ACT(ft[:, g * 4 + cc, :], r_[:, :], AF.Square, (rB,), (ftB[g * 4 + cc],))
              for hf in range(2):
                  banks = [next_ps() for _ in range(4)]
                  for q in range(4):
                      w, wB = get_group(l, J, 31 + hf * 4 + q)
                      for cc in range(4):
                          for kk in range(8):
                              MM(banks[cc][0][:, :], w[:, kk * 512 + cc * 128: kk * 512 + cc * 128 + 128],
                                 ft[:, q * 8 + kk, :], q == 0 and kk == 0, q == 3 and kk == 7,
                                 (wB, ftB[q * 8 + kk]), (banks[cc][1],))
                  for cc in range(4):
                      c = hf * 4 + cc
                      STT("dve", xt[:, c, :], banks[cc][0][:, :], ada[:, 40 + c:41 + c], xt[:, c, :], ALU.mult, ALU.add,
                          (banks[cc][1], adaB, xtB[c]), (xtB[c],))
              if last and final_norm:
                  rms_stats(J)
                  for c in range(NCH):
                      STT("dve", xt[:, c, :], xt[:, c, :], fg[:, c:c + 1], rstd[:, :], ALU.mult, ALU.mult,
                          (xtB[c], constB, rstdB), (xtB[c],))
              dst = outT if last else xres
              P.dma("act", dst.rearrange("(c p) t -> p c t", p=128)[:, :, tsl], xt[:, :, :], tuple(xtB),
                    (xresB[J],) if not last else ())

    except _Stop:
        P.dma("act", outT.rearrange("(c p) t -> p c t", p=128)[:, :, 0:T], xt[:, :, :], tuple(xtB), ())
    if debug_stop is None:
        assert wstate["used"] == len(use_seq)

    import contextlib
    with contextlib.ExitStack() as es:
        sems = {e: es.enter_context(nc.semaphore(f"s_{e}")) for e in Prog.ENGS}
        dsems = {e: [es.enter_context(nc.semaphore(f"d_{e}{i}")) for i in range(Prog.RING)] for e in ("sp", "pool", "act")}
        block = es.enter_context(nc.Block())
        P.emit(nc, block, sems, dsems)
    counts = {e: len(P.lists[e]) for e in Prog.ENGS}
    return nc, counts


def _pc(v, nchunk):
    return np.ascontiguousarray(np.asarray(v, np.float32).reshape(nchunk, 128).T)


def make_consts():
    ident = np.eye(128, dtype=np.float32)
    perm = np.zeros((128, 128), np.float32)
    for m in range(128):
        if (m % 64) < 32:
            perm[m + 32, m] = -1.0
        else:
            perm[m - 32, m] = 1.0
    tri = (np.arange(128)[:, None] <= np.arange(128)[None, :]).astype(np.float32)
    inv_freq = (10000.0 ** (-np.arange(0, 64, 2, dtype=np.float32) / 64)).astype(np.float32)
    invf = (inv_freq[np.arange(128) % 32] / np.float32(2 * math.pi)).astype(np.float32)[:, None]
    return np.ascontiguousarray(np.concatenate([ident, perm, tri, invf], axis=1))


def make_pvec(inp, L):
    pv = np.zeros((L, 128, NPV), np.float32)
    lam = np.zeros((L, 128, 256), np.float32)
    for l in range(L):
        pv[l, :, PV_GMIX:PV_GMIX + 8] = _pc(inp["norm_mix_g"][l], 8)
        pv[l, :, PV_GMLP:PV_GMLP + 8] = _pc(inp["norm_mlp_g"][l], 8)
        caw = np.asarray(inp["conv_a_w"][l], np.float32)
        pv[l, :, PV_CAW:PV_CAW + 124] = caw.T.reshape(4, 128, 31).transpose(1, 0, 2).reshape(128, 124)
        pv[l, :, PV_CAB:PV_CAB + 4] = _pc(inp["conv_a_b"][l], 4)
        pv[l, :, PV_LNG:PV_LNG + 4] = _pc(inp["ln_a_g"][l], 4)
        pv[l, :, PV_LNB:PV_LNB + 4] = _pc(inp["ln_a_b"][l], 4)
        cbw = np.asarray(inp["conv_b_w"][l], np.float32)
        pv[l, :, PV_CBW:PV_CBW + 12] = cbw.T.reshape(4, 128, 3).transpose(1, 0, 2).reshape(128, 12)
        pv[l, :, PV_SUBG] = np.asarray(inp["subln_g"][l], np.float32)
        pv[l, :, PV_BADA:PV_BADA + 48] = _pc(inp["b_ada"][l], 48)
        row = np.concatenate([inp["lam_q1"][l], inp["lam_q2"][l], inp["lam_k1"][l], inp["lam_k2"][l]]).astype(np.float32)
        lam[l] = np.broadcast_to(row[None, :], (128, 256))
    return pv, lam


_CACHE = {}


def run_layers(inp, xT_list, S, L, final_norm, lambda_layer0=0, ncores=None):
    key = (S, L, final_norm, lambda_layer0)
    if key not in _CACHE:
        _CACHE[key] = build_program(S, L, final_norm, lambda_layer0)[0]
    nc = _CACHE[key]
    n = len(xT_list)
    consts = make_consts()
    pv, lam = make_pvec(inp, L)
    fgl = _pc(inp["final_g"], 8)
    f32 = lambda a: np.ascontiguousarray(np.asarray(a, np.float32))
    shared = {k: f32(inp[k]) for k in ("w_ada", "w_in", "w_a_out", "w_b_out", "w_c_out", "w_out", "w_ff1", "w_ff2")}
    in_maps = []
    for b in range(n):
        m = dict(shared)
        m["xT"] = xT_list[b]
        m["cT"] = _pc(np.asarray(inp["c"])[b], 8)
        m["posr"] = np.ascontiguousarray(np.broadcast_to(np.asarray(inp["positions"])[b].astype(np.int32)[None, :], (128, S)))
        m["consts"] = consts
        m["pvec"] = pv
        m["lamv"] = lam
        m["finalg"] = fgl
        in_maps.append(m)
    res = run_bass_kernel_spmd(nc, in_maps, core_ids=list(range(n)))
    return [np.asarray(r["outT"]) for r in res.results]


def kernel(**inputs):
    x = np.asarray(inputs["x"], np.float32)
    B, S, _ = x.shape
    L = int(np.asarray(inputs["w_in"]).shape[0])
    xT_list = [np.ascontiguousarray(x[b].T) for b in range(B)]
    outs = run_layers(inputs, xT_list, S, L, True)
    return np.ascontiguousarray(np.stack([o.T for o in outs], axis=0)).astype(np.float32)
```

```python
import math
import numpy as np
import concourse.bass as bass
import concourse.mybir as mybir
from concourse.bass_utils import run_bass_kernel_spmd

F32, BF16, I32 = mybir.dt.float32, mybir.dt.bfloat16, mybir.dt.int32
AF = mybir.ActivationFunctionType
ALU = mybir.AluOpType
AX = mybir.AxisListType

D = 1024
NCH = 8
T = 512
DFF = 4096
DIN = 8704
NG = 39
NPV = 8 + 8 + 124 + 4 + 4 + 4 + 12 + 1 + 48
PV_GMIX, PV_GMLP, PV_CAW, PV_CAB, PV_LNG, PV_LNB, PV_CBW, PV_SUBG, PV_BADA = 0, 8, 16, 140, 144, 148, 152, 164, 165


class Buf:
    __slots__ = ("name", "w", "rd", "rd_dma")

    def __init__(self, name):
        self.name = name
        self.w = None
        self.rd = {}
        self.rd_dma = []


class Ins:
    __slots__ = ("eng", "fn", "deps", "sig", "idx", "dma", "ev", "seq")


class Prog:
    ENGS = ("pe", "act", "dve", "pool", "sp")
    RING = 8

    def __init__(self):
        self.lists = {e: [] for e in self.ENGS}
        self.dma_n = {e: 0 for e in self.ENGS}
        self.dma_hist = {e: [] for e in self.ENGS}
        self.seq = 0

    def _add(self, eng, fn, reads, writes, dma):
        ins = Ins()
        ins.eng, ins.fn, ins.sig, ins.idx, ins.dma, ins.ev = eng, fn, False, 0, dma, None
        ins.seq = self.seq
        self.seq += 1
        deps = {}
        for b in reads:
            if b.w is not None:
                deps[id(b.w)] = b.w
            other = "dve" if eng == "pool" else ("pool" if eng == "dve" else None)
            if other is not None and other in b.rd:
                deps[id(b.rd[other])] = b.rd[other]
        for b in writes:
            if b.w is not None:
                deps[id(b.w)] = b.w
            for r in b.rd.values():
                deps[id(r)] = r
            for r in b.rd_dma:
                deps[id(r)] = r
        if dma:
            k = self.dma_n[eng]
            self.dma_n[eng] = k + 1
            ins.ev = (eng, k % self.RING, 16 * (k // self.RING + 1))
            hist = self.dma_hist[eng]
            if k >= self.RING:
                p = hist[k - self.RING]
                deps[id(p)] = p
            hist.append(ins)
        for b in reads:
            if dma:
                b.rd_dma.append(ins)
            else:
                b.rd[eng] = ins
        for b in writes:
            b.w = ins
            b.rd = {}
            b.rd_dma = []
        dl = []
        for d in deps.values():
            if d is ins:
                continue
            if (not d.dma) and (not dma) and d.eng == "pe" and eng == "pe":
                continue
            if not d.dma:
                d.sig = True
            dl.append(d)
        ins.deps = dl
        self.lists[eng].append(ins)
        return ins

    def op(self, eng, fn, reads=(), writes=()):
        return self._add(eng, fn, reads, writes, False)

    def dma(self, q, out, in_, reads=(), writes=()):
        return self._add(q, lambda E: E.dma_start(out=out, in_=in_), reads, writes, True)

    @staticmethod
    def barrier(new_bufs, old_bufs):
        rd = {}
        rd_dma = []
        for b in old_bufs:
            cands = list(b.rd.values())
            if b.w is not None:
                if b.w.dma:
                    rd_dma.append(b.w)
                else:
                    cands.append(b.w)
            for c in cands:
                if c.eng not in rd or rd[c.eng].seq < c.seq:
                    rd[c.eng] = c
            rd_dma.extend(b.rd_dma)
        for nb in new_bufs:
            nb.w = None
            nb.rd = dict(rd)
            nb.rd_dma = list(rd_dma)

    def emit(self, nc, block, sems, dsems):
        for e in self.ENGS:
            n = 0
            for ins in self.lists[e]:
                if ins.sig and not ins.dma:
                    n += 1
                    ins.idx = n

        def event(d):
            if d.dma:
                q, slot, val = d.ev
                return dsems[q][slot], val
            return sems[d.eng], d.idx

        def body_for(ename):
            def body(E):
                waited = {}
                for ins in self.lists[ename]:
                    for d in ins.deps:
                        sem, val = event(d)
                        if waited.get(id(sem), 0) < val:
                            E.wait_ge(sem, val)
                            waited[id(sem)] = val
                    r = ins.fn(E)
                    if ins.dma:
                        r.then_inc(dsems[ename][ins.ev[1]], 16)
                    elif ins.sig:
                        r.then_inc(sems[ename], 1)
                hist = self.dma_hist[ename]
                for ins in hist[-self.RING:]:
                    sem, val = event(ins)
                    if waited.get(id(sem), 0) < val:
                        E.wait_ge(sem, val)
                        waited[id(sem)] = val
            return body

        block.tensor(body_for("pe"))
        block.scalar(body_for("act"))
        block.vector(body_for("dve"))
        block.gpsimd(body_for("pool"))
        block.sync(body_for("sp"))


def group_table():
    g = {}
    for i in range(17):
        g[i] = ("w_in", 0, 8, i * 512, 512)
    g[17] = ("w_a_out", 0, 4, 0, 1024)
    g[18] = ("w_b_out", 0, 4, 0, 1024)
    g[19] = ("w_c_out", 0, 8, 0, 512)
    g[20] = ("w_c_out", 0, 8, 512, 512)
    g[21] = ("w_out", 0, 8, 0, 512)
    g[22] = ("w_out", 0, 8, 512, 512)
    for i in range(8):
        g[23 + i] = ("w_ff1", 0, 8, i * 512, 512)
    for hf in range(2):
        for q in range(4):
            g[31 + hf * 4 + q] = ("w_ff2", q * 1024, 8, hf * 512, 512)
    return g


TILE_SEQ = ([1, 0, 3, 4, 2, 5, 6, 7, 8, 9, 10] +
            [11, 17, 12, 17, 13, 18, 14, 18, 15, 19, 16, 20, 21, 22] +
            list(range(23, 31)) + list(range(31, 39)))


class _Stop(Exception):
    pass


USE_WCACHE = False
CAST_ENG = "act"


def build_program(S, L, final_norm=True, lambda_layer0=0, debug_stop=None, debug_tile=0):
    NT = S // T
    NTB = S // 128
    nc = bass.Bass("TRN2", target_bir_lowering=False)
    P = Prog()
    GT = group_table()

    def din(name, shape, dt=F32):
        return nc.dram_tensor(name, shape, dt, kind="ExternalInput").ap()

    xT = din("xT", [D, S])
    cT = din("cT", [128, NCH])
    posr = din("posr", [128, S], I32)
    consts = din("consts", [128, 3 * 128 + 1])
    pvec = din("pvec", [L, 128, NPV])
    lamv = din("lamv", [L, 128, 256])
    finalg = din("finalg", [128, NCH])
    wd = {
        "w_ada": din("w_ada", [L, D, 6 * D]),
        "w_in": din("w_in", [L, D, DIN]),
        "w_a_out": din("w_a_out", [L, 512, D]),
        "w_b_out": din("w_b_out", [L, 512, D]),
        "w_c_out": din("w_c_out", [L, D, D]),
        "w_out": din("w_out", [L, D, D]),
        "w_ff1": din("w_ff1", [L, D, DFF]),
        "w_ff2": din("w_ff2", [L, DFF, D]),
    }
    outT = nc.dram_tensor("outT", [D, S], F32, kind="ExternalOutput").ap()

    def dscr(name, shape, dt):
        return nc.dram_tensor(name, shape, dt, kind="Internal").ap()

    xres = dscr("xres", [D, S], F32)
    wsc = dscr("wsc", [L * NG, 128, 4096], BF16)
    kscr = dscr("kscr", [NCH, 128, S], BF16)
    vscr = dscr("vscr", [8, 128, NTB * 129], BF16)
    csscr = dscr("csscr", [2, 128, S], F32)

    off = [20 * 1024]

    def sb(name, shape, dt, at=None):
        esz = 2 if dt == BF16 else 4
        nbytes = esz * int(np.prod(shape[1:]))
        nbytes = (nbytes + 63) // 64 * 64
        if at is None:
            at = off[0]
            off[0] += nbytes
        t = nc.alloc_sbuf_tensor_at(name, list(shape), dt, offset=at)
        return t, at, nbytes

    wr = [sb(f"wr{i}", [128, 4096], BF16)[0] for i in range(4)]
    wrB = [Buf(f"wr{i}") for i in range(4)]
    ws = [sb(f"ws{i}", [128, 2048], F32)[0] for i in range(2)]
    wsB = [Buf(f"ws{i}") for i in range(2)]
    xt = sb("xt", [128, NCH, T], F32)[0]
    xtB = [Buf(f"xt{c}") for c in range(NCH)]
    ht = sb("ht", [128, NCH, T], BF16)[0]
    htB = [Buf(f"ht{c}") for c in range(NCH)]
    sq = sb("sq", [128, NCH, T], BF16)[0]
    sqB = [Buf(f"sq{c}") for c in range(NCH)]
    rstd = sb("rstd", [128, T], F32)[0]
    rstdB = Buf("rstd")
    NT32 = 4
    t32 = [sb(f"t32_{i}", [128, T], F32)[0] for i in range(NT32)]
    t32B = [Buf(f"t32_{i}") for i in range(NT32)]
    cs = sb("cs", [128, T], F32)[0]
    sn = sb("sn", [128, T], F32)[0]
    csB, snB = Buf("cs"), Buf("sn")
    pv = sb("pv", [128, NPV], F32)[0]
    pvB = Buf("pv")
    lamt = sb("lamt", [128, 256], F32)[0]
    lamp = sb("lamp", [128, 2, 64], F32)[0]
    lams = sb("lams", [128, 8], F32)[0]
    lamB = Buf("lam")
    ada = sb("ada", [128, 48], F32)[0]
    gm = sb("gm", [128, 16], F32)[0]
    adaB = Buf("ada")
    cact = sb("cact", [128, NCH], F32)[0]
    cactB = Buf("cact")
    cst = sb("cst", [128, 3 * 128 + 1], F32)[0]
    cstB = Buf("cst")
    onesD = sb("onesD", [128, 128], BF16)[0]
    ones512 = sb("ones512", [128, 128], F32)[0]
    ident = sb("ident", [128, 128], BF16)[0]
    perm = sb("perm", [128, 128], BF16)[0]
    tri = sb("tri", [128, 128], BF16)[0]
    epsb = sb("epsb", [128, 1], F32)[0]
    negpi = sb("negpi", [128, 1], F32)[0]
    fg = sb("fg", [128, NCH], F32)[0]
    constB = Buf("const")
    zb16 = sb("zb16", [128, T], BF16)[0]
    zf32 = sb("zf32", [128, 32], F32)[0]
    ahalo = sb("ahalo", [128, 4, 30], F32)[0]
    uhalo = sb("uhalo", [128, 4, 2], F32)[0]
    ahB = [Buf(f"ah{c}") for c in range(4)]
    uhB = [Buf(f"uh{c}") for c in range(4)]
    pint = sb("pint", [128, T], I32)[0]
    pintB = Buf("pint")
    kr = [sb(f"kr{i}", [128, T], BF16)[0] for i in range(4)]
    vr = [sb(f"vr{i}", [128, 4 * 129], BF16)[0] for i in range(4)]
    krB = [Buf(f"kr{i}") for i in range(4)]
    vrB = [Buf(f"vr{i}") for i in range(4)]
    pT = [sb(f"pT{i}", [128, T], BF16)[0] for i in range(3)]
    pTB = [Buf(f"pT{i}") for i in range(3)]
    ofin = sb("ofin", [128, 2, 128], F32)[0]
    ofinB = [Buf("ofin0"), Buf("ofin1")]
    osq = sb("osq", [128, 128], F32)[0]
    osqB = Buf("osq")
    osm = sb("osm", [128, 8], F32)[0]
    osmB = Buf("osm")
    onb = [sb(f"onb{i}", [128, 128], BF16)[0] for i in range(2)]
    onbB = [Buf("onb0"), Buf("onb1")]
    carry0 = off[0]
    Qm = [sb(f"Qm{i}", [128, NCH, T], BF16)[0] for i in range(2)]
    QmB = [[Buf(f"Qm{i}_{c}") for c in range(NCH)] for i in range(2)]
    oT = sb("oT", [128, 8, T], BF16)[0]
    oTB = [Buf(f"oT{h}") for h in range(8)]
    aact = sb("aact", [128, 4, T], BF16)[0]
    aactB = [Buf(f"aact{c}") for c in range(4)]
    bmix = sb("bmix", [128, 4, T], BF16)[0]
    bmixB = [Buf(f"bmix{c}") for c in range(4)]
    carry_end = off[0]
    ft = sb("ft", [128, 32, T], BF16, at=carry0)[0]
    ftB = [Buf(f"ft{c}") for c in range(32)]
    assert carry_end - carry0 == 32 * T * 2, (carry_end - carry0)
    carryB = [b for q in QmB for b in q] + oTB + aactB + bmixB
    arena0 = off[0]
    abuf = sb("abuf", [128, 4, 30 + T], F32)[0]
    abufB = [Buf(f"abuf{c}") for c in range(4)]
    acc = sb("acc", [128, 4, T], F32)[0]
    accB = [Buf(f"acc{c}") for c in range(4)]
    cgs = sb("cgs", [128, 4, T], F32)[0]
    cgsB = [Buf(f"cgs{c}") for c in range(4)]
    ubuf = sb("ubuf", [128, 4, 2 + T], F32)[0]
    ubufB = [Buf(f"ubuf{c}") for c in range(4)]
    qtmp = [sb(f"qtmp{i}", [128, T], BF16)[0] for i in range(2)]
    qtmpB = [Buf("qtmp0"), Buf("qtmp1")]
    kt = sb("kt", [128, NCH, T], BF16)[0]
    ktB = [Buf(f"kt{c}") for c in range(NCH)]
    vt = sb("vt", [128, 8, 4, 129], BF16)[0]
    vtB = [Buf(f"vt{i}") for i in range(8)]
    arena_end = off[0]
    mg = sb("mg", [128, NCH, T], F32, at=arena0)[0]
    mgb = sb("mgb", [128, NCH, T], BF16, at=arena0 + NCH * T * 4)[0]
    mgB = [Buf(f"mg{c}") for c in range(NCH)]
    mgbB = [Buf(f"mgb{c}") for c in range(NCH)]
    assert arena_end - arena0 >= NCH * T * 6
    arenaB = abufB + accB + cgsB + ubufB + qtmpB + ktB + vtB
    assert off[0] <= 224 * 1024, off[0]

    ps = [nc.alloc_psum_tensor(f"ps{i}", [128, 512], F32) for i in range(7)]
    psB = [Buf(f"ps{i}") for i in range(7)]
    ptr = nc.alloc_psum_tensor("ptr", [128, 1024], BF16)
    ptrB = Buf("ptr")
    ps_rr = [0]

    def next_ps():
        i = ps_rr[0] % 7
        ps_rr[0] += 1
        return ps[i], psB[i]

    t32_rr = [0]

    def next_t32():
        i = t32_rr[0] % NT32
        t32_rr[0] += 1
        return t32[i], t32B[i]

    xresB = [Buf(f"xres{j}") for j in range(NT)]
    kregB = [Buf(f"kreg{j}") for j in range(NT)]
    vregB = [Buf(f"vreg{j}") for j in range(NT)]
    csregB = [Buf(f"csreg{j}") for j in range(NT)]
    wscB = {}

    def MM(out, lhsT, rhs, start, stop, rd, wr_):
        P.op("pe", lambda E: E.matmul(out, lhsT, rhs, start=start, stop=stop), rd, wr_)

    def ACT(out, in_, func, rd, wr_, **kw):
        P.op("act", lambda E: E.activation(out=out, in_=in_, func=func, **kw), rd, wr_)

    def TT(eng, out, in0, in1, op, rd, wr_):
        P.op(eng, lambda E: E.tensor_tensor(out=out, in0=in0, in1=in1, op=op), rd, wr_)

    def TS(eng, out, in0, s1, s2, op0, op1, rd, wr_):
        P.op(eng, lambda E: E.tensor_scalar(out=out, in0=in0, scalar1=s1, scalar2=s2, op0=op0, op1=op1), rd, wr_)

    def TSS(eng, out, in_, s, op, rd, wr_):
        P.op(eng, lambda E: E.tensor_single_scalar(out=out, in_=in_, scalar=s, op=op), rd, wr_)

    def STT(eng, out, in0, scalar, in1, op0, op1, rd, wr_):
        P.op(eng, lambda E: E.scalar_tensor_tensor(out=out, in0=in0, scalar=scalar, in1=in1, op0=op0, op1=op1), rd, wr_)

    def CP(eng, out, in_, rd, wr_):
        if eng == "act":
            P.op(eng, lambda E: E.activation(out=out, in_=in_, func=AF.Copy), rd, wr_)
        else:
            P.op(eng, lambda E: E.tensor_copy(out=out, in_=in_), rd, wr_)

    def RECIP(out, in_, rd, wr_):
        P.op("dve", lambda E: E.reciprocal(out=out, in_=in_), rd, wr_)

    def RSUM(out, in_, rd, wr_):
        P.op("dve", lambda E: E.reduce_sum(out=out, in_=in_, axis=AX.X), rd, wr_)

    def MEMSET(eng, ap, val, wr_):
        P.op(eng, lambda E: E.memset(ap, val), (), wr_)

    use_seq = []
    for l in range(L):
        for J in range(NT):
            for gid in TILE_SEQ:
                use_seq.append((l, J, gid))
    wstate = {"loaded": 0, "used": 0, "stage": 0}
    LOOKAHEAD = 2

    def src_piece(l, gid, hh):
        name, row0, nkc, col0, width = GT[gid]
        w = wd[name][l]
        hk = nkc // 2
        r0 = row0 + hh * hk * 128
        return w[r0:r0 + hk * 128, col0:col0 + width].rearrange("(k p) n -> p k n", p=128), hk, width

    def record_load(i):
        l, J, gid = use_seq[i]
        slot = i % 4
        key = (l, gid)
        if (key not in wscB) or not USE_WCACHE:
            wscB[key] = Buf(f"wsc{l}_{gid}")
            for hh in range(2):
                src, hk, width = src_piece(l, gid, hh)
                si = wstate["stage"] % 2
                wstate["stage"] += 1
                P.dma("sp", ws[si][:, :].rearrange("p (k n) -> p k n", k=hk), src, (), (wsB[si],))
                CP(("pool", "dve")[hh], wr[slot][:, hh * 2048:(hh + 1) * 2048], ws[si][:, :], (wsB[si],), (wrB[slot],))
            if USE_WCACHE:
                P.dma("act", wsc[l * NG + gid], wr[slot][:, :], (wrB[slot],), (wscB[key],))
        else:
            for qq in range(4):
                P.dma("sp", wr[slot][:, qq * 1024:(qq + 1) * 1024], wsc[l * NG + gid][:, qq * 1024:(qq + 1) * 1024],
                      (wscB[key],), (wrB[slot],))

    def get_group(l, J, gid):
        i = wstate["used"]
        assert use_seq[i] == (l, J, gid), (use_seq[i], (l, J, gid))
        lim = min(i + 1 + LOOKAHEAD, len(use_seq))
        while wstate["loaded"] < lim:
            record_load(wstate["loaded"])
            wstate["loaded"] += 1
        wstate["used"] = i + 1
        slot = i % 4
        return wr[slot], wrB[slot]

    P.dma("sp", cst[:, :], consts, (), (cstB,))
    CP("dve", ident[:, :], cst[:, 0:128], (cstB,), (constB,))
    CP("dve", perm[:, :], cst[:, 128:256], (cstB,), (constB,))
    CP("dve", tri[:, :], cst[:, 256:384], (cstB,), (constB,))
    MEMSET("pool", onesD[:, :], 1.0 / D, (constB,))
    MEMSET("pool", ones512[:, :], 1.0 / 512, (constB,))
    MEMSET("pool", epsb[:, :], 1e-6, (constB,))
    MEMSET("pool", negpi[:, :], -math.pi, (constB,))
    MEMSET("pool", zb16[:, :], 0.0, (constB,))
    MEMSET("pool", zf32[:, :], 0.0, (constB,))
    P.dma("sp", fg[:, :], finalg, (), (constB,))
    P.dma("sp", cact[:, :], cT, (), (cactB,))
    ACT(cact[:, :], cact[:, :], AF.Silu, (cactB,), (cactB,))
    invf = cst[:, 384:385]
    for j in range(NT):
        P.dma("sp", pint[:, :], posr[:, j * T:(j + 1) * T], (), (pintB,))
        yb, ybB = next_t32()
        CP("dve", yb[:, :], pint[:, :], (pintB,), (ybB,))
        for which, addc in ((0, 0.75), (1, 0.5)):
            y2, y2B = next_t32()
            TS("dve", y2[:, :], yb[:, :], invf, addc, ALU.mult, ALU.add, (ybB, cstB), (y2B,))
            CP("dve", pint[:, :], y2[:, :], (y2B,), (pintB,))
            y3, y3B = next_t32()
            CP("dve", y3[:, :], pint[:, :], (pintB,), (y3B,))
            TT("dve", y2[:, :], y2[:, :], y3[:, :], ALU.subtract, (y2B, y3B), (y2B,))
            TSS("dve", y3[:, :], y2[:, :], 0.0, ALU.is_lt, (y2B,), (y3B,))
            TT("dve", y2[:, :], y2[:, :], y3[:, :], ALU.add, (y2B, y3B), (y2B,))
            ACT(y2[:, :], y2[:, :], AF.Sin, (y2B, constB), (y2B,), scale=2 * math.pi, bias=negpi[:, :])
            P.dma("sp", csscr[which][:, j * T:(j + 1) * T], y2[:, :], (y2B,), (csregB[j],))
    for c in range(4):
        MEMSET("pool", ahalo[:, c, :], 0.0, (ahB[c],))
        MEMSET("pool", uhalo[:, c, :], 0.0, (uhB[c],))

    def rms_stats(J):
        for c in range(NCH):
            ACT(sq[:, c, :], xt[:, c, :], AF.Square, (xtB[c],), (sqB[c],))
        pn, pnB = next_ps()
        for c in range(NCH):
            MM(pn[:, :], onesD[:, :], sq[:, c, :], c == 0, c == NCH - 1, (constB, sqB[c]), (pnB,))
        ACT(rstd[:, :], pn[:, :], AF.Sqrt, (pnB, constB), (rstdB,), bias=epsb[:, :])
        RECIP(rstd[:, :], rstd[:, :], (rstdB,), (rstdB,))

    def norm_mod(J, gcol, shcol):
        rms_stats(J)
        for c in range(NCH):
            tb, tbB = next_t32()
            STT("dve", tb[:, :], xt[:, c, :], gm[:, gcol + c:gcol + c + 1], rstd[:, :], ALU.mult, ALU.mult,
                (xtB[c], adaB, rstdB), (tbB,))
            ACT(ht[:, c, :], tb[:, :], AF.Identity, (tbB, adaB), (htB[c],), bias=ada[:, shcol + c:shcol + c + 1])

    def proj(w, wB, cc, width=512, nkc=8, rhs_t=None, rhsB=None, coff=0):
        p_, pB = next_ps()
        rhs_t = ht if rhs_t is None else rhs_t
        rhsB = htB if rhsB is None else rhsB
        for kc in range(nkc):
            MM(p_[:, :], w[:, kc * width + coff + cc * 128: kc * width + coff + cc * 128 + 128], rhs_t[:, kc, :],
               kc == 0, kc == nkc - 1, (wB, rhsB[kc]), (pB,))
        return p_, pB

    try:
      for l in range(L):
          lam_init = 0.8 - 0.6 * math.exp(-0.3 * (l + lambda_layer0))
          P.dma("sp", pv[:, :], pvec[l], (), (pvB,))
          P.dma("sp", lamt[:, :], lamv[l], (), (lamB,))
          TT("dve", lamp[:, :, :], lamt[:, 0:128].rearrange("p (a b) -> p a b", a=2),
             lamt[:, 128:256].rearrange("p (a b) -> p a b", a=2), ALU.mult, (lamB,), (lamB,))
          RSUM(lams[:, 0:2], lamp[:, :, :], (lamB,), (lamB,))
          ACT(lams[:, 0:2], lams[:, 0:2], AF.Exp, (lamB,), (lamB,))
          TT("dve", lams[:, 2:3], lams[:, 0:1], lams[:, 1:2], ALU.subtract, (lamB,), (lamB,))
          TS("dve", lams[:, 3:4], lams[:, 2:3], lam_init, -1.0, ALU.add, ALU.mult, (lamB,), (lamB,))
          TSS("dve", lams[:, 4:5], pv[:, PV_SUBG:PV_SUBG + 1], 1.0 - lam_init, ALU.mult, (pvB, lamB), (lamB,))
          nlam = lams[:, 3:4]
          sgc = lams[:, 4:5]
          pa, paB = next_ps()
          for pc in range(24):
              si = wstate["stage"] % 2
              wstate["stage"] += 1
              src = wd["w_ada"][l][:, pc * 256:(pc + 1) * 256].rearrange("(k p) n -> p k n", p=128)
              P.dma("sp", ws[si][:, :].rearrange("p (k n) -> p k n", k=8), src, (), (wsB[si],))
              for cc in range(2):
                  j = pc * 2 + cc
                  for kc in range(8):
                      MM(pa[:, j:j + 1], ws[si][:, kc * 256 + cc * 128: kc * 256 + cc * 128 + 128], cact[:, kc:kc + 1],
                         kc == 0, kc == 7, (wsB[si], cactB), (paB,))
          TT("dve", ada[:, :], pa[:, 0:48], pv[:, PV_BADA:PV_BADA + 48], ALU.add, (paB, pvB), (adaB,))
          STT("dve", gm[:, 0:8], ada[:, 8:16], 1.0, pv[:, PV_GMIX:PV_GMIX + 8], ALU.add, ALU.mult, (adaB, pvB), (adaB,))
          STT("dve", gm[:, 8:16], ada[:, 32:40], 1.0, pv[:, PV_GMLP:PV_GMLP + 8], ALU.add, ALU.mult, (adaB, pvB), (adaB,))
          for c in range(4):
              MEMSET("pool", ahalo[:, c, :], 0.0, (ahB[c],))
              MEMSET("pool", uhalo[:, c, :], 0.0, (uhB[c],))

          xsrc = xT if l == 0 else xres
          last = (l == L - 1)

          for J in range(NT):
              tsl = slice(J * T, (J + 1) * T)
              Prog.barrier(arenaB, mgB + mgbB)
              Prog.barrier(carryB, ftB)
              P.dma("sp", xt[:, :, :], xsrc.rearrange("(c p) t -> p c t", p=128)[:, :, tsl],
                    (xresB[J],) if l > 0 else (), tuple(xtB))
              P.dma("sp", cs[:, :], csscr[0][:, tsl], (csregB[J],), (csB,))
              P.dma("sp", sn[:, :], csscr[1][:, tsl], (csregB[J],), (snB,))
              MEMSET("pool", Qm[0][64:128, :, :], 0.0, tuple(QmB[0]))
              MEMSET("pool", Qm[1][0:64, :, :], 0.0, tuple(QmB[1]))
              norm_mod(J, 0, 0)
              if debug_stop == 1 and J == debug_tile:
                  raise _Stop()

              w2, w2B = get_group(l, J, 1)
              w1, w1B = get_group(l, J, 0)
              for c in range(4):
                  CP("pool", abuf[:, c, 0:30], ahalo[:, c, :], (ahB[c],), (abufB[c],))
                  pA, pAB = proj(w2, w2B, c)
                  pB_, pBB = proj(w1, w1B, c)
                  sg_, sgB = next_t32()
                  ACT(sg_[:, :], pA[:, :], AF.Sigmoid, (pAB,), (sgB,))
                  TT("dve", abuf[:, c, 30:30 + T], pB_[:, :], sg_[:, :], ALU.mult, (pBB, sgB), (abufB[c],))
              if debug_stop == 11 and J == debug_tile:
                  raise _Stop()
              caw = lambda c, k: pv[:, PV_CAW + c * 31 + k: PV_CAW + c * 31 + k + 1]
              for k in range(31):
                  for c in range(4):
                      eng = "dve"
                      if k == 0:
                          TS(eng, acc[:, c, :], abuf[:, c, 0:T], caw(c, 0), pv[:, PV_CAB + c:PV_CAB + c + 1],
                             ALU.mult, ALU.add, (abufB[c], pvB), (accB[c],))
                      else:
                          STT(eng, acc[:, c, :], abuf[:, c, k:k + T], caw(c, k), acc[:, c, :], ALU.mult, ALU.add,
                              (abufB[c], pvB, accB[c]), (accB[c],))
              for c in range(4):
                  CP("pool", ahalo[:, c, :], abuf[:, c, T:T + 30], (abufB[c], accB[c]), (ahB[c],))

              if debug_stop == 2 and J == debug_tile:
                  raise _Stop()
              wc, wcB = get_group(l, J, 3)
              for c in range(4):
                  p_, pB = proj(wc, wcB, c)
                  ACT(cgs[:, c, :], p_[:, :], AF.Copy, (pB,), (cgsB[c],))
              wx, wxB = get_group(l, J, 4)
              cbw = lambda c, k: pv[:, PV_CBW + c * 3 + k: PV_CBW + c * 3 + k + 1]
              for c in range(4):
                  CP("pool", ubuf[:, c, 0:2], uhalo[:, c, :], (uhB[c],), (ubufB[c],))
                  p_, pB = proj(wx, wxB, c)
                  TT("dve", ubuf[:, c, 2:2 + T], p_[:, :], cgs[:, c, :], ALU.mult, (pB, cgsB[c]), (ubufB[c],))
                  TSS("dve", cgs[:, c, :], ubuf[:, c, 0:T], cbw(c, 0), ALU.mult, (ubufB[c], pvB), (cgsB[c],))
                  for k in (1, 2):
                      STT("dve", cgs[:, c, :], ubuf[:, c, k:k + T], cbw(c, k), cgs[:, c, :], ALU.mult, ALU.add,
                          (ubufB[c], pvB, cgsB[c]), (cgsB[c],))
                  CP("pool", uhalo[:, c, :], ubuf[:, c, T:T + 2], (ubufB[c],), (uhB[c],))
              wg, wgB = get_group(l, J, 2)
              for c in range(4):
                  p_, pB = proj(wg, wgB, c)
                  TT("dve", bmix[:, c, :], p_[:, :], cgs[:, c, :], ALU.mult, (pB, cgsB[c]), (bmixB[c],))

              if debug_stop == 3 and J == debug_tile:
                  raise _Stop()
              for which, gids in (("q", (5, 6)), ("k", (7, 8))):
                  for gi, gid in enumerate(gids):
                      w, wB = get_group(l, J, gid)
                      for c in range(4):
                          ch = gi * 4 + c
                          p_, pB = proj(w, wB, c)
                          qi = ch % 2
                          ACT(qtmp[qi][:, :], p_[:, :], AF.Copy, (pB,), (qtmpB[qi],))
                          p2, p2B = next_ps()
                          MM(p2[:, :], perm[:, :], qtmp[qi][:, :], True, True, (constB, qtmpB[qi]), (p2B,))
                          t1, t1B = next_t32()
                          t2, t2B = next_t32()
                          TT("dve", t1[:, :], p_[:, :], cs[:, :], ALU.mult, (pB, csB), (t1B,))
                          TT("dve", t2[:, :], p2[:, :], sn[:, :], ALU.mult, (p2B, snB), (t2B,))
                          if which == "q":
                              TT("dve", Qm[0][0:64, ch, :], t1[0:64, :], t2[0:64, :], ALU.add, (t1B, t2B), (QmB[0][ch],))
                              TT("dve", Qm[1][64:128, ch, :], t1[64:128, :], t2[64:128, :], ALU.add, (t1B, t2B), (QmB[1][ch],))
                          else:
                              TT("dve", kt[:, ch, :], t1[:, :], t2[:, :], ALU.add, (t1B, t2B), (ktB[ch],))
              P.dma("act", kscr.rearrange("c p t -> p c t")[:, :, tsl], kt[:, :, :], tuple(ktB), (kregB[J],))

              if debug_stop == 4 and J == debug_tile:
                  raise _Stop()
              MEMSET("pool", vt[:, :, :, 128:129], 1.0, tuple(vtB))
              for gi, gid in enumerate((9, 10)):
                  w, wB = get_group(l, J, gid)
                  for bq in range(4):
                      p_, pB = next_ps()
                      for kc in range(8):
                          MM(p_[:, :], ht[:, kc, bq * 128:(bq + 1) * 128], w[:, kc * 512:(kc + 1) * 512],
                             kc == 0, kc == 7, (wB, htB[kc]), (pB,))
                      src = p_[:, :].rearrange("p (h d) -> p h d", h=4)
                      dst = vt[:, gi * 4:gi * 4 + 4, bq, 0:128]
                      if bq % 2 == 0:
                          ACT(dst, src, AF.Copy, (pB,), (vtB[gi * 4 + bq],))
                      else:
                          CP("dve", dst, src, (pB,), (vtB[gi * 4 + bq],))
              P.dma("act", vscr.rearrange("h p e -> p h e")[:, :, J * 516:(J + 1) * 516],
                    vt[:, :, :, :].rearrange("p h b e -> p h (b e)"), tuple(vtB), (vregB[J],))

              if debug_stop == 5 and J == debug_tile:
                  raise _Stop()
              pm, pmB = next_ps()
              pq, pqB = next_ps()
              for c in range(4):
                  a2, a2B = next_t32()
                  ACT(a2[:, :], acc[:, c, :], AF.Square, (accB[c],), (a2B,))
                  MM(pm[:, :], ones512[:, :], acc[:, c, :], c == 0, c == 3, (constB, accB[c]), (pmB,))
                  MM(pq[:, :], ones512[:, :], a2[:, :], c == 0, c == 3, (constB, a2B), (pqB,))
              mean, meanB = next_t32()
              ACT(mean[:, :], pm[:, :], AF.Copy, (pmB,), (meanB,))
              rsa, rsaB = next_t32()
              TT("dve", rsa[:, :], mean[:, :], mean[:, :], ALU.mult, (meanB,), (rsaB,))
              TT("dve", rsa[:, :], pq[:, :], rsa[:, :], ALU.subtract, (pqB, rsaB), (rsaB,))
              ACT(rsa[:, :], rsa[:, :], AF.Sqrt, (rsaB, constB), (rsaB,), bias=epsb[:, :])
              RECIP(rsa[:, :], rsa[:, :], (rsaB,), (rsaB,))
              for c in range(4):
                  TT("dve", acc[:, c, :], acc[:, c, :], mean[:, :], ALU.subtract, (accB[c], meanB), (accB[c],))
                  TT("dve", acc[:, c, :], acc[:, c, :], rsa[:, :], ALU.mult, (accB[c], rsaB), (accB[c],))
                  ACT(aact[:, c, :], acc[:, c, :], AF.Silu, (accB[c], pvB), (aactB[c],),
                      scale=pv[:, PV_LNG + c:PV_LNG + c + 1], bias=pv[:, PV_LNB + c:PV_LNB + c + 1])

              if debug_stop == 6 and J == debug_tile:
                  raise _Stop()
              pieces = [(h, comp, j) for h in range(8) for comp in range(2) for j in range(J + 1)]
              pstate = {"loaded": 0}

              def load_piece(i):
                  h, comp, j = pieces[i]
                  s = i % 4
                  P.dma("sp", kr[s][:, :], kscr[h][:, j * T:(j + 1) * T], (kregB[j],), (krB[s],))
                  P.dma("sp", vr[s][:, :], vscr[h][:, j * 516:(j + 1) * 516], (vregB[j],), (vrB[s],))

              pi = 0
              sidx = 0
              for h in range(8):
                  pend = None

                  def do_pv(st):
                      j, comp, bi, s, pti, r = st
                      i = 4 * j + bi
                      for b in range(max(r, 0), 4):
                          MM(ps[b][:, comp * 129:(comp + 1) * 129], pT[pti][:, b * 128:(b + 1) * 128],
                             vr[s][:, bi * 129:(bi + 1) * 129], i == 0, i == 4 * J + b,
                             (pTB[pti], vrB[s]), (psB[b],))

                  for comp in range(2):
                      for j in range(J + 1):
                          lim = min(pi + 3, len(pieces))
                          while pstate["loaded"] < lim:
                              load_piece(pstate["loaded"])
                              pstate["loaded"] += 1
                          s = pi % 4
                          pi += 1
                          for bi in range(4):
                              r = bi if j == J else -1
                              c0 = 128 * max(r, 0)
                              sb_i = 4 + sidx % 3
                              pti = sidx % 3
                              sidx += 1
                              MM(ps[sb_i][:, c0:T], kr[s][:, bi * 128:(bi + 1) * 128], Qm[comp][:, h, c0:T], True, True,
                                 (krB[s], QmB[comp][h]), (psB[sb_i],))
                              ACT(pT[pti][:, c0:T], ps[sb_i][:, c0:T], AF.Exp, (psB[sb_i],), (pTB[pti],), scale=0.125)
                              if r >= 0:
                                  TT("dve", pT[pti][:, c0:c0 + 128], pT[pti][:, c0:c0 + 128], tri[:, :], ALU.mult,
                                     (pTB[pti], constB), (pTB[pti],))
                              if pend is not None:
                                  do_pv(pend)
                              pend = (j, comp, bi, s, pti, r)
                  do_pv(pend)
                  for b in range(4):
                      ov = ps[b][:, 0:258].rearrange("p (c e) -> p c e", c=2)
                      RECIP(osm[:, 0:2], ov[:, :, 128], (psB[b],), (osmB,))
                      TT("dve", osm[:, 2:3], osm[:, 1:2], nlam, ALU.mult, (osmB, lamB), (osmB,))
                      TSS("dve", ofin[:, 0, :], ps[b][:, 0:128], osm[:, 0:1], ALU.mult, (psB[b], osmB), (ofinB[0],))
                      STT("dve", ofin[:, 1, :], ps[b][:, 129:257], osm[:, 2:3], ofin[:, 0, :], ALU.mult, ALU.add,
                          (psB[b], osmB, ofinB[0]), (ofinB[1],))
                      ACT(osq[:, :], ofin[:, 1, :], AF.Square, (ofinB[1],), (osqB,))
                      RSUM(osm[:, 3:4], osq[:, :], (osqB,), (osmB,))
                      ACT(osm[:, 4:5], osm[:, 3:4], AF.Sqrt, (osmB, constB), (osmB,), scale=1.0 / 128, bias=epsb[:, :])
                      RECIP(osm[:, 5:6], osm[:, 4:5], (osmB,), (osmB,))
                      TSS("dve", onb[b % 2][:, :], ofin[:, 1, :], osm[:, 5:6], ALU.mult, (ofinB[1], osmB), (onbB[b % 2],))
                      P.op("pe", lambda E, b=b: E.transpose(ptr[:, b * 128:(b + 1) * 128], onb[b % 2][:, :], ident[:, :]),
                           (onbB[b % 2], constB), (ptrB,))
                  ACT(oT[:, h, :], ptr[:, 0:T], AF.Copy, (ptrB, lamB), (oTB[h],), scale=sgc)

              if debug_stop == 7 and J == debug_tile:
                  raise _Stop()
              Prog.barrier(mgB + mgbB, arenaB)
              for br, (g0, gy, act_t, actB, nk, wdt) in enumerate((
                      (11, 17, aact, aactB, 4, 1024), (13, 18, bmix, bmixB, 4, 1024), (15, 19, oT, oTB, 8, 512))):
                  for hf in range(2):
                      wg_, wgB_ = get_group(l, J, g0 + hf)
                      wy, wyB = get_group(l, J, gy + (hf if br == 2 else 0))
                      for cc in range(4):
                          c = hf * 4 + cc
                          pG, pGB = proj(wg_, wgB_, cc)
                          coff = 0 if br == 2 else hf * 512
                          pY, pYB = proj(wy, wyB, cc, width=wdt, nkc=nk, rhs_t=act_t, rhsB=actB, coff=coff)
                          sg_, sgB = next_t32()
                          ACT(sg_[:, :], pG[:, :], AF.Sigmoid, (pGB,), (sgB,))
                          if br == 0:
                              TT("dve", mg[:, c, :], pY[:, :], sg_[:, :], ALU.mult, (pYB, sgB), (mgB[c],))
                          else:
                              TT("dve", sg_[:, :], pY[:, :], sg_[:, :], ALU.mult, (pYB, sgB), (sgB,))
                              if br == 1:
                                  TT("dve", mg[:, c, :], mg[:, c, :], sg_[:, :], ALU.add, (mgB[c], sgB), (mgB[c],))
                              else:
                                  TT("dve", mgb[:, c, :], mg[:, c, :], sg_[:, :], ALU.add, (mgB[c], sgB), (mgbB[c],))
              for hf in range(2):
                  w, wB = get_group(l, J, 21 + hf)
                  for cc in range(4):
                      c = hf * 4 + cc
                      p_, pB = proj(w, wB, cc, rhs_t=mgb, rhsB=mgbB)
                      STT("dve", xt[:, c, :], p_[:, :], ada[:, 16 + c:17 + c], xt[:, c, :], ALU.mult, ALU.add,
                          (pB, adaB, xtB[c]), (xtB[c],))

              if debug_stop == 8 and J == debug_tile:
                  raise _Stop()
              Prog.barrier(ftB, carryB)
              norm_mod(J, 8, 24)
              for g in range(8):
                  w, wB = get_group(l, J, 23 + g)
                  for cc in range(4):
                      p_, pB = proj(w, wB, cc)
                      r_, rB = next_t32()
                      ACT(r_[:, :], p_[:, :], AF.Relu, (pB,), (rB,))
                      ACT(ft[:, g * 4 + cc, :], r_[:, :], AF.Square, (rB,), (ftB[g * 4 + cc],))
              for hf in range(2):
                  banks = [next_ps() for _ in range(4)]
                  for q in range(4):
                      w, wB = get_group(l, J, 31 + hf * 4 + q)
                      for cc in range(4):
                          for kk in range(8):
                              MM(banks[cc][0][:, :], w[:, kk * 512 + cc * 128: kk * 512 + cc * 128 + 128],
                                 ft[:, q * 8 + kk, :], q == 0 and kk == 0, q == 3 and kk == 7,
                                 (wB, ftB[q * 8 + kk]), (banks[cc][1],))
                  for cc in range(4):
                      c = hf * 4 + cc
                      STT("dve", xt[:, c, :], banks[cc][0][:, :], ada[:, 40 + c:41 + c], xt[:, c, :], ALU.mult, ALU.add,
                          (banks[cc][1], adaB, xtB[c]), (xtB[c],))
              if last and final_norm:
                  rms_stats(J)
                  for c in range(NCH):
                      STT("dve", xt[:, c, :], xt[:, c, :], fg[:, c:c + 1], rstd[:, :], ALU.mult, ALU.mult,
                          (xtB[c], constB, rstdB), (xtB[c],))
              dst = outT if last else xres
              P.dma("act", dst.rearrange("(c p) t -> p c t", p=128)[:, :, tsl], xt[:, :, :], tuple(xtB),
                    (xresB[J],) if not last else ())

    except _Stop:
        P.dma("act", outT.rearrange("(c p) t -> p c t", p=128)[:, :, 0:T], xt[:, :, :], tuple(xtB), ())
    if debug_stop is None:
        assert wstate["used"] == len(use_seq)

    import contextlib
    with contextlib.ExitStack() as es:
        sems = {e: es.enter_context(nc.semaphore(f"s_{e}")) for e in Prog.ENGS}
        dsems = {e: [es.enter_context(nc.semaphore(f"d_{e}{i}")) for i in range(Prog.RING)] for e in ("sp", "pool", "act")}
        block = es.enter_context(nc.Block())
        P.emit(nc, block, sems, dsems)
    counts = {e: len(P.lists[e]) for e in Prog.ENGS}
    return nc, counts


def _pc(v, nchunk):
    return np.ascontiguousarray(np.asarray(v, np.float32).reshape(nchunk, 128).T)


def make_consts():
    ident = np.eye(128, dtype=np.float32)
    perm = np.zeros((128, 128), np.float32)
    for m in range(128):
        if (m % 64) < 32:
            perm[m + 32, m] = -1.0
        else:
            perm[m - 32, m] = 1.0
    tri = (np.arange(128)[:, None] <= np.arange(128)[None, :]).astype(np.float32)
    inv_freq = (10000.0 ** (-np.arange(0, 64, 2, dtype=np.float32) / 64)).astype(np.float32)
    invf = (inv_freq[np.arange(128) % 32] / np.float32(2 * math.pi)).astype(np.float32)[:, None]
    return np.ascontiguousarray(np.concatenate([ident, perm, tri, invf], axis=1))


def make_pvec(inp, L):
    pv = np.zeros((L, 128, NPV), np.float32)
    lam = np.zeros((L, 128, 256), np.float32)
    for l in range(L):
        pv[l, :, PV_GMIX:PV_GMIX + 8] = _pc(inp["norm_mix_g"][l], 8)
        pv[l, :, PV_GMLP:PV_GMLP + 8] = _pc(inp["norm_mlp_g"][l], 8)
        caw = np.asarray(inp["conv_a_w"][l], np.float32)
        pv[l, :, PV_CAW:PV_CAW + 124] = caw.T.reshape(4, 128, 31).transpose(1, 0, 2).reshape(128, 124)
        pv[l, :, PV_CAB:PV_CAB + 4] = _pc(inp["conv_a_b"][l], 4)
        pv[l, :, PV_LNG:PV_LNG + 4] = _pc(inp["ln_a_g"][l], 4)
        pv[l, :, PV_LNB:PV_LNB + 4] = _pc(inp["ln_a_b"][l], 4)
        cbw = np.asarray(inp["conv_b_w"][l], np.float32)
        pv[l, :, PV_CBW:PV_CBW + 12] = cbw.T.reshape(4, 128, 3).transpose(1, 0, 2).reshape(128, 12)
        pv[l, :, PV_SUBG] = np.asarray(inp["subln_g"][l], np.float32)
        pv[l, :, PV_BADA:PV_BADA + 48] = _pc(inp["b_ada"][l], 48)
        row = np.concatenate([inp["lam_q1"][l], inp["lam_q2"][l], inp["lam_k1"][l], inp["lam_k2"][l]]).astype(np.float32)
        lam[l] = np.broadcast_to(row[None, :], (128, 256))
    return pv, lam


_CACHE = {}


def run_layers(inp, xT_list, S, L, final_norm, lambda_layer0=0, ncores=None):
    key = (S, L, final_norm, lambda_layer0)
    if key not in _CACHE:
        _CACHE[key] = build_program(S, L, final_norm, lambda_layer0)[0]
    nc = _CACHE[key]
    n = len(xT_list)
    consts = make_consts()
    pv, lam = make_pvec(inp, L)
    fgl = _pc(inp["final_g"], 8)
    f32 = lambda a: np.ascontiguousarray(np.asarray(a, np.float32))
    shared = {k: f32(inp[k]) for k in ("w_ada", "w_in", "w_a_out", "w_b_out", "w_c_out", "w_out", "w_ff1", "w_ff2")}
    in_maps = []
    for b in range(n):
        m = dict(shared)
        m["xT"] = xT_list[b]
        m["cT"] = _pc(np.asarray(inp["c"])[b], 8)
        m["posr"] = np.ascontiguousarray(np.broadcast_to(np.asarray(inp["positions"])[b].astype(np.int32)[None, :], (128, S)))
        m["consts"] = consts
        m["pvec"] = pv
        m["lamv"] = lam
        m["finalg"] = fgl
        in_maps.append(m)
    res = run_bass_kernel_spmd(nc, in_maps, core_ids=list(range(n)))
    return [np.asarray(r["outT"]) for r in res.results]


def kernel(**inputs):
    x = np.asarray(inputs["x"], np.float32)
    B, S, _ = x.shape
    L = int(np.asarray(inputs["w_in"]).shape[0])
    xT_list = [np.ascontiguousarray(x[b].T) for b in range(B)]
    outs = run_layers(inputs, xT_list, S, L, True)
    return np.ascontiguousarray(np.stack([o.T for o in outs], axis=0)).astype(np.float32)
```
